# Optimizing a Trainium2 kernel written in Bass

```python
import jax, jax.numpy as jnp
from jax import lax
import numpy as np

D_MODEL = 2048
BATCH = 16
SEQ = 2048
DEPTH = 4

GRID_W = 64
CTX_LEN = 256
N_EVEN = (DEPTH + 1) // 2
N_ODD = DEPTH // 2

POOL_WINDOWS = (2, 4, 8, 16)
N_POOL = 4
POOL_W = D_MODEL // 2
POOL_GROUP = POOL_W // N_POOL
RWKV_W = D_MODEL // 2
RWKV_HEAD = 64
RWKV_HEADS = RWKV_W // RWKV_HEAD
LORA_W = 64
GN_EPS = 64e-5
ATT_HEADS = 16
HEAD_DIM = D_MODEL // ATT_HEADS
KV_HEADS = 4
GROUP = ATT_HEADS // KV_HEADS
ATT_W = ATT_HEADS * HEAD_DIM
KV_W = KV_HEADS * HEAD_DIM
AXIS_DIM = HEAD_DIM // 2
ROPE_THETA = 10000.0
Q_BLOCK = 128
EV_GATE_W = POOL_W + RWKV_W
EV_IN = POOL_W + 3 * RWKV_W + 4 * LORA_W + EV_GATE_W
EV_SPLITS = (POOL_W, POOL_W + RWKV_W, POOL_W + 2 * RWKV_W, POOL_W + 3 * RWKV_W,
             POOL_W + 3 * RWKV_W + 2 * LORA_W, POOL_W + 3 * RWKV_W + 4 * LORA_W)
OD_IN = ATT_W + 2 * KV_W + ATT_W
DEEPNORM_ALPHA = (2 * DEPTH) ** 0.25
DEEPNORM_BETA = (8 * DEPTH) ** -0.25

kernel_name = 'hybrid_pool_rwkv7_gqa_diffusion_trunk'


def layer_norm(x, g, b, eps=1e-6):
    xf = x.astype(jnp.float32)
    mu = jnp.mean(xf, -1, keepdims=True)
    var = jnp.mean(jnp.square(xf - mu), -1, keepdims=True)
    return ((xf - mu) * lax.rsqrt(var + eps)).astype(x.dtype) * g + b


def rms_norm(x, g, eps=1e-6):
    xf = x.astype(jnp.float32)
    return (xf * lax.rsqrt(jnp.mean(jnp.square(xf), -1, keepdims=True) + eps)).astype(x.dtype) * g


def axial_rope_tables(rows):
    row = jnp.repeat(jnp.arange(rows, dtype=jnp.float32), GRID_W)
    col = jnp.tile(jnp.arange(GRID_W, dtype=jnp.float32), rows)
    inv_freq = ROPE_THETA ** (-jnp.arange(0, AXIS_DIM, 2, dtype=jnp.float32) / AXIS_DIM)
    ang = jnp.stack([row[:, None] * inv_freq, col[:, None] * inv_freq], axis=1)
    return jnp.cos(ang), jnp.sin(ang)


def apply_axial_rope(x, cos, sin):
    B, T, H, _ = x.shape
    xr = x.astype(jnp.float32).reshape(B, T, H, 2, 2, AXIS_DIM // 2)
    x1, x2 = xr[..., 0, :], xr[..., 1, :]
    c = cos[None, :, None]
    s = sin[None, :, None]
    out = jnp.stack([x1 * c - x2 * s, x2 * c + x1 * s], axis=-2)
    return out.reshape(x.shape).astype(x.dtype)


def attention(q, k, v):
    s = jnp.einsum('bqhgd,bkhd->bhgqk', q, k, preferred_element_type=jnp.float32) * (HEAD_DIM ** -0.5)
    p = jax.nn.softmax(s, axis=-1).astype(v.dtype)
    return jnp.einsum('bhgqk,bkhd->bqhgd', p, v)


def blocked_attention(q, k, v):
    B, T = q.shape[:2]
    qb = q.reshape(B, T // Q_BLOCK, Q_BLOCK, KV_HEADS, GROUP, HEAD_DIM).swapaxes(0, 1)
    ob = lax.map(lambda qi: attention(qi, k, v), qb)
    return ob.swapaxes(0, 1).reshape(B, T, KV_HEADS, GROUP, HEAD_DIM)


def multiscale_pool(u, w_grp, scale):
    B, T, _ = u.shape
    ug = u.astype(jnp.float32).reshape(B, T, N_POOL, POOL_GROUP)
    cs = jnp.concatenate([jnp.zeros((B, 1, N_POOL, POOL_GROUP), jnp.float32), jnp.cumsum(ug, axis=1)], axis=1)
    t = jnp.arange(T)
    means = []
    for g, w in enumerate(POOL_WINDOWS):
        lo = jnp.clip(t - w // 2, 0, T)
        hi = jnp.clip(t - w // 2 + w, 0, T)
        csg = cs[:, :, g]
        cnt = (hi - lo).astype(jnp.float32)[None, :, None]
        means.append((csg[:, hi] - csg[:, lo]) / cnt)
    pooled = (jnp.stack(means, axis=2) - ug).astype(u.dtype)
    mixed = jnp.einsum('btgc,gcd->btgd', pooled, w_grp)
    return mixed.reshape(B, T, POOL_W) * scale


def token_shift(u, mu, reverse):
    if reverse:
        nb = jnp.pad(u[:, 1:], ((0, 0), (0, 1), (0, 0)))
    else:
        nb = jnp.pad(u[:, :-1], ((0, 0), (1, 0), (0, 0)))
    return u + (nb - u) * mu


def rwkv_dir_inputs(r0, k0, v0, wd, ad, mu_rkv, mu_lora, w0, w2, a0, a2, k_k, k_a, reverse):
    B, T, _ = r0.shape
    sh = lambda u, mu: token_shift(u.astype(jnp.float32), mu, reverse)
    hd = lambda u: u.reshape(B, T, RWKV_HEADS, RWKV_HEAD)
    r = sh(r0, mu_rkv[0])
    k = sh(k0, mu_rkv[1])
    v = sh(v0, mu_rkv[2])
    w_log = -jax.nn.softplus(-(w0 + jnp.tanh(sh(wd, mu_lora[0])) @ w2)) - 0.5
    decay = jnp.exp(-jnp.exp(w_log))
    a = jax.nn.sigmoid(a0 + sh(ad, mu_lora[1]) @ a2)
    kk = hd(k * k_k)
    kk = kk * lax.rsqrt(jnp.sum(jnp.square(kk), -1, keepdims=True) + 1e-12)
    k = k * (1.0 + (a - 1.0) * k_a)
    return (hd(r), hd(decay), hd(k), hd(v), -kk, kk * hd(a))


def wkv7_scan(state, inputs, reverse):
    def step(S, xs):
        r, w, k, v, a, b = xs
        sa = jnp.einsum('bhvk,bhk->bhv', S, a)
        S = S * w[:, :, None, :] + sa[..., None] * b[:, :, None, :] + v[..., None] * k[:, :, None, :]
        return S, jnp.einsum('bhvk,bhk->bhv', S, r)
    xs = tuple(jnp.swapaxes(u, 0, 1) for u in inputs)
    state, y = lax.scan(step, state, xs, reverse=reverse)
    return state, jnp.swapaxes(y, 0, 1)


def rwkv_readout(y, inputs, r_k, gn_g, gn_b):
    r, _, k, v, _, _ = inputs
    B, T = y.shape[:2]
    mu = jnp.mean(y, -1, keepdims=True)
    var = jnp.mean(jnp.square(y - mu), -1, keepdims=True)
    yn = ((y - mu) * lax.rsqrt(var + GN_EPS)).reshape(B, T, RWKV_W) * gn_g + gn_b
    bonus = jnp.sum(r * k * r_k, -1, keepdims=True) * v
    return yn + bonus.reshape(B, T, RWKV_W)


def split_even(p):
    a_in, r0, k0, v0, wd, ad, gate = jnp.split(p, EV_SPLITS, axis=-1)
    lead = p.shape[:-1]
    return a_in, r0, k0, v0, wd.reshape(*lead, 2, LORA_W), ad.reshape(*lead, 2, LORA_W), gate


def even_mixer(hl, hc, w_in, w_out, pool_w, pool_scale, mu_rkv, mu_lora, w0, w2, a0, a2,
               k_k, k_a, r_k, gn_g, gn_b, need_ctx_out):
    B = hl.shape[0]
    al, rl, kl, vl, wdl, adl, gl = split_even(hl @ w_in)
    ac, rc, kc, vc, wdc, adc, gc = split_even(hc @ w_in)
    outs_l, outs_c = [], []
    for d, reverse in enumerate((False, True)):
        dir_p = (mu_rkv[d], mu_lora[d], w0[d], w2[d], a0[d], a2[d], k_k, k_a)
        in_c = rwkv_dir_inputs(rc, kc, vc, wdc[..., d, :], adc[..., d, :], *dir_p, reverse)
        in_l = rwkv_dir_inputs(rl, kl, vl, wdl[..., d, :], adl[..., d, :], *dir_p, reverse)
        s0 = jnp.zeros((B, RWKV_HEADS, RWKV_HEAD, RWKV_HEAD), jnp.float32)
        s_ctx, y_c = wkv7_scan(s0, in_c, reverse)
        _, y_l = wkv7_scan(s_ctx, in_l, reverse)
        outs_l.append(rwkv_readout(y_l, in_l, r_k, gn_g[d], gn_b[d]))
        if need_ctx_out:
            outs_c.append(rwkv_readout(y_c, in_c, r_k, gn_g[d], gn_b[d]))
    ol = (outs_l[0] + outs_l[1]).astype(hl.dtype)
    yl = (jnp.concatenate([multiscale_pool(al, pool_w, pool_scale), ol], -1) * jax.nn.silu(gl)) @ w_out
    if not need_ctx_out:
        return yl, None
    oc = (outs_c[0] + outs_c[1]).astype(hc.dtype)
    yc = (jnp.concatenate([multiscale_pool(ac, pool_w, pool_scale), oc], -1) * jax.nn.silu(gc)) @ w_out
    return yl, yc


def odd_mixer(hl, hc, w_in, w_out, q_g, k_g, cos, sin, need_ctx_out):
    B, T, _ = hl.shape
    L = hc.shape[1]
    ql, kl, vl, gl = jnp.split(hl @ w_in, (ATT_W, ATT_W + KV_W, ATT_W + 2 * KV_W), axis=-1)
    ql = apply_axial_rope(rms_norm(ql.reshape(B, T, ATT_HEADS, HEAD_DIM), q_g), cos, sin)
    kl = apply_axial_rope(rms_norm(kl.reshape(B, T, KV_HEADS, HEAD_DIM), k_g), cos, sin)
    vl = vl.reshape(B, T, KV_HEADS, HEAD_DIM)
    kc, vc = jnp.split(hc @ w_in[:, ATT_W:ATT_W + 2 * KV_W], 2, axis=-1)
    kc = rms_norm(kc.reshape(B, L, KV_HEADS, HEAD_DIM), k_g)
    vc = vc.reshape(B, L, KV_HEADS, HEAD_DIM)
    k_all = jnp.concatenate([kc, kl], axis=1)
    v_all = jnp.concatenate([vc, vl], axis=1)
    ol = blocked_attention(ql.reshape(B, T, KV_HEADS, GROUP, HEAD_DIM), k_all, v_all)
    yl = (ol.reshape(B, T, ATT_W) * jax.nn.silu(gl)) @ w_out
    if not need_ctx_out:
        return yl, None
    qc = rms_norm((hc @ w_in[:, :ATT_W]).reshape(B, L, ATT_HEADS, HEAD_DIM), q_g)
    gc = hc @ w_in[:, ATT_W + 2 * KV_W:]
    oc = attention(qc.reshape(B, L, KV_HEADS, GROUP, HEAD_DIM), kc, vc)
    yc = (oc.reshape(B, L, ATT_W) * jax.nn.silu(gc)) @ w_out
    return yl, yc


def setup_inputs(seed: int = 0) -> dict:
    key = jax.random.key(seed)
    ks = iter(jax.random.split(key, 32))
    f32 = jnp.float32
    nrm = lambda shape, s=1.0: jax.random.normal(next(ks), shape, f32) * s
    uni = lambda shape: jax.random.uniform(next(ks), shape, f32)
    D = D_MODEL
    return {
        'x': nrm((BATCH, SEQ, D)),
        'c': nrm((BATCH, D)),
        'ctx': nrm((BATCH, CTX_LEN, D)),
        'c_ctx': nrm((D,)),
        'mod_w': nrm((DEPTH, D, 3 * D), 0.5 * D ** -0.5),
        'mod_b': nrm((DEPTH, 3 * D), 0.01),
        'ln_g': 1.0 + nrm((DEPTH, D), 0.05),
        'ln_b': nrm((DEPTH, D), 0.01),
        'ev_w_in': nrm((N_EVEN, D, EV_IN), D ** -0.5),
        'ev_w_out': nrm((N_EVEN, EV_GATE_W, D), DEEPNORM_BETA * EV_GATE_W ** -0.5),
        'pool_w': nrm((N_EVEN, N_POOL, POOL_GROUP, POOL_GROUP), POOL_GROUP ** -0.5),
        'pool_scale': 1.0 + nrm((N_EVEN, POOL_W), 0.1),
        'rw_mu_rkv': uni((N_EVEN, 2, 3, RWKV_W)),
        'rw_mu_lora': uni((N_EVEN, 2, 2, LORA_W)),
        'rw_w0': jnp.linspace(-6.0, 1.0, RWKV_W, dtype=f32) + nrm((N_EVEN, 2, RWKV_W), 0.1),
        'rw_w2': nrm((N_EVEN, 2, LORA_W, RWKV_W), 0.5 * LORA_W ** -0.5),
        'rw_a0': nrm((N_EVEN, 2, RWKV_W), 0.1),
        'rw_a2': nrm((N_EVEN, 2, LORA_W, RWKV_W), 0.5 * LORA_W ** -0.5),
        'rw_k_k': 0.85 + nrm((N_EVEN, RWKV_W), 0.05),
        'rw_k_a': 1.0 + nrm((N_EVEN, RWKV_W), 0.05),
        'rw_r_k': nrm((N_EVEN, RWKV_HEADS, RWKV_HEAD), 0.1),
        'rw_gn_g': 1.0 + nrm((N_EVEN, 2, RWKV_W), 0.05),
        'rw_gn_b': nrm((N_EVEN, 2, RWKV_W), 0.01),
        'od_w_in': nrm((N_ODD, D, OD_IN), D ** -0.5),
        'od_w_out': nrm((N_ODD, ATT_W, D), DEEPNORM_BETA * ATT_W ** -0.5),
        'q_norm_g': 1.0 + nrm((N_ODD, HEAD_DIM), 0.05),
        'k_norm_g': 1.0 + nrm((N_ODD, HEAD_DIM), 0.05),
    }


def reference(x, c, ctx, c_ctx, mod_w, mod_b, ln_g, ln_b, ev_w_in, ev_w_out, pool_w, pool_scale,
              rw_mu_rkv, rw_mu_lora, rw_w0, rw_w2, rw_a0, rw_a2, rw_k_k, rw_k_a, rw_r_k,
              rw_gn_g, rw_gn_b, od_w_in, od_w_out, q_norm_g, k_norm_g):
    ROWS = x.shape[1] // GRID_W
    cos, sin = axial_rope_tables(ROWS)
    sc = jax.nn.silu(c)
    scc = jax.nn.silu(c_ctx)
    xl, xc = x, ctx
    for l in range(DEPTH):
        last = l == DEPTH - 1
        i = l // 2
        shift_l, scale_l, gate_l = jnp.split((sc @ mod_w[l] + mod_b[l])[:, None, :], 3, axis=-1)
        shift_c, scale_c, gate_c = jnp.split(scc @ mod_w[l] + mod_b[l], 3, axis=-1)
        hl = xl * (1.0 + scale_l) + shift_l
        hc = xc * (1.0 + scale_c) + shift_c
        if l % 2 == 0:
            yl, yc = even_mixer(hl, hc, ev_w_in[i], ev_w_out[i], pool_w[i], pool_scale[i],
                                rw_mu_rkv[i], rw_mu_lora[i], rw_w0[i], rw_w2[i], rw_a0[i], rw_a2[i],
                                rw_k_k[i], rw_k_a[i], rw_r_k[i], rw_gn_g[i], rw_gn_b[i], not last)
        else:
            yl, yc = odd_mixer(hl, hc, od_w_in[i], od_w_out[i], q_norm_g[i], k_norm_g[i], cos, sin, not last)
        xl = layer_norm(DEEPNORM_ALPHA * xl + gate_l * yl, ln_g[l], ln_b[l])
        if not last:
            xc = layer_norm(DEEPNORM_ALPHA * xc + gate_c * yc, ln_g[l], ln_b[l])
    return xl
```

```python
import contextlib
import numpy as np
import concourse.bass as bass
import concourse.mybir as mybir

F32 = mybir.dt.float32
BF16 = mybir.dt.bfloat16
F32R = mybir.dt.float32r
AF = mybir.ActivationFunctionType
ALU = mybir.AluOpType
AX = mybir.AxisListType

SEG = 30000
NDSEM = 12
ENGS = ("pe", "act", "dve", "pool", "sp")


class _Op:
    __slots__ = ("eng", "fn", "waits", "inc", "semval", "snap", "dma", "dwaits")

    def __init__(self, eng, fn, dma):
        self.eng = eng
        self.fn = fn
        self.waits = []
        self.dwaits = []
        self.inc = False
        self.semval = None
        self.snap = None
        self.dma = dma


class Sched:
    def __init__(self, nc):
        self.nc = nc
        self.ops = {e: [] for e in ENGS}
        self.ndma = {e: 0 for e in ENGS}
        self.dma_ops = {e: [] for e in ENGS}
        self.writers = {}
        self.readers = {}
        self.known = {e: {} for e in ENGS}
        self.kdma = {e: set() for e in ENGS}

    def _need(self, op, ev):
        e = op.eng
        if ev[0] == "dma":
            _, q, i = ev
            if (q, i) in self.kdma[e]:
                return
            op.dwaits.append((q, i))
            self.kdma[e].add((q, i))
            src = self.dma_ops[q][i]
        else:
            src_e, n = ev
            if src_e == e and e == "pe":
                return
            if self.known[e].get(src_e, 0) >= n:
                return
            op.waits.append((src_e, n))
            src = self.ops[src_e][n - 1]
            src.inc = True
            self.known[e][src_e] = n
        if src.snap is not None:
            k = self.known[e]
            for se, n2 in src.snap.items():
                if k.get(se, 0) < n2:
                    k[se] = n2

    def _record(self, eng, fn, reads, writes, dma=False):
        op = _Op(eng, fn, None)
        if dma:
            i = self.ndma[eng]
            op.dma = i
            self.ndma[eng] += 1
            self.dma_ops[eng].append(op)
            if i >= NDSEM:
                self._need(op, ("dma", eng, i - NDSEM))
        for k in reads:
            for ev in self.writers.get(k, ()):
                self._need(op, ev)
        for k in writes:
            for ev in self.writers.get(k, ()):
                self._need(op, ev)
            for ev in self.readers.get(k, ()):
                self._need(op, ev)
        self.ops[eng].append(op)
        n = len(self.ops[eng])
        ev = ("dma", eng, op.dma) if dma else (eng, n)
        for k in writes:
            self.writers[k] = [ev]
            self.readers[k] = []
        for k in reads:
            if k in writes:
                continue
            lst = self.readers.setdefault(k, [])
            if ev[0] != "dma":
                lst[:] = [x for x in lst if x[0] != ev[0]]
            lst.append(ev)
        if not dma:
            self.known[eng][eng] = n if eng == "pe" else self.known[eng].get(eng, 0)
        op.snap = dict(self.known[eng])
        return op

    @staticmethod
    def _key(ap):
        return ap.tensor.name

    def _keys(self, aps):
        out = []
        for a in aps:
            if a is None or isinstance(a, (int, float)):
                continue
            if isinstance(a, str) or isinstance(a, tuple):
                out.append(a)
            else:
                out.append(self._key(a))
        return out

    def op(self, eng, method, reads, writes, *args, **kwargs):
        rk = self._keys(reads)
        wk = self._keys(writes)

        def fn(e, method=method, args=args, kwargs=kwargs):
            return getattr(e, method)(*args, **kwargs)

        return self._record(eng, fn, rk, wk)

    def matmul(self, out, lhsT, rhs, start=True, stop=True, **kw):
        return self.op("pe", "matmul", [lhsT, rhs], [out], out, lhsT=lhsT, rhs=rhs, start=start, stop=stop, **kw)

    def transpose(self, out, in_, ident):
        return self.op("pe", "transpose", [in_, ident], [out], out, in_, ident)

    def act(self, out, in_, func, bias=None, scale=None, accum_out=None, eng="act"):
        kw = {}
        rd = [in_]
        if bias is not None:
            kw["bias"] = bias
            rd.append(bias)
        if scale is not None:
            kw["scale"] = scale
            rd.append(scale)
        wr = [out]
        if accum_out is not None:
            kw["accum_out"] = accum_out
            wr.append(accum_out)
        return self.op(eng, "activation", rd, wr, out, in_, func, **kw)

    def tt(self, eng, out, in0, in1, op):
        return self.op(eng, "tensor_tensor", [in0, in1], [out], out, in0, in1, op)

    def ts(self, eng, out, in0, s1, op0, s2=None, op1=None, accum_out=None):
        kw = {}
        if op1 is not None:
            kw["op1"] = op1
        wr = [out]
        if accum_out is not None:
            kw["accum_out"] = accum_out
            wr.append(accum_out)
        return self.op(eng, "tensor_scalar", [in0, s1, s2], wr, out, in0, s1, s2, op0, **kw)

    def stt(self, out, in0, scalar, in1, op0, op1, eng="dve"):
        return self.op(eng, "scalar_tensor_tensor", [in0, scalar, in1], [out], out, in0, scalar, in1, op0, op1)

    def copy(self, eng, out, in_):
        if eng == "act":
            return self.op("act", "copy", [in_], [out], out, in_)
        return self.op(eng, "tensor_copy", [in_], [out], out, in_)

    def memset(self, eng, ap, val):
        return self.op(eng, "memset", [], [ap], ap, val)

    def dma(self, out, in_, q="sp", rkeys=None, wkeys=None, **kw):
        rk = self._keys(rkeys if rkeys is not None else [in_])
        wk = self._keys(wkeys if wkeys is not None else [out])

        def fn(e, out=out, in_=in_, kw=kw):
            return e.dma_start(out=out, in_=in_, **kw)

        return self._record(q, fn, rk, wk, dma=True)

    def begin(self, stack, nseg=8):
        nc = self.nc
        self.csems = {e: [stack.enter_context(nc.semaphore(f"c_{e}_{j}")) for j in range(nseg)] for e in ENGS}
        self.dsems = {e: [stack.enter_context(nc.semaphore(f"d_{e}_{j}")) for j in range(NDSEM)] for e in ("sp", "act", "pool")}
        self.tinc = {e: 0 for e in ENGS}
        self.tdma = {e: 0 for e in ENGS}

    def flush(self):
        nc = self.nc
        csems, dsems = self.csems, self.dsems
        base_inc = dict(self.tinc)
        base_dma = dict(self.tdma)
        for e in ENGS:
            c = base_inc[e]
            for op in self.ops[e]:
                if op.inc and op.dma is None:
                    c += 1
                    op.semval = c
            self.tinc[e] = c
            assert c < SEG * len(csems[e]), "out of compute semaphore segments"
        ops = self.ops
        ndma = self.ndma

        def run(engname, e):
            for op in ops[engname]:
                for (se, n) in op.waits:
                    v = ops[se][n - 1].semval - 1
                    e.wait_ge(csems[se][v // SEG], v % SEG + 1)
                for (q, i) in op.dwaits:
                    g = base_dma[q] + i
                    e.wait_ge(dsems[q][g % NDSEM], 16 * (g // NDSEM + 1))
                ins = op.fn(e)
                if op.dma is not None:
                    g = base_dma[engname] + op.dma
                    ins.then_inc(dsems[engname][g % NDSEM], 16)
                elif op.inc:
                    v = op.semval - 1
                    ins.then_inc(csems[engname][v // SEG], 1)
            n = ndma[engname]
            for i in range(max(0, n - NDSEM), n):
                g = base_dma[engname] + i
                e.wait_ge(dsems[engname][g % NDSEM], 16 * (g // NDSEM + 1))

        with nc.Block() as block:
            @block.tensor
            def _(e):
                run("pe", e)

            @block.scalar
            def _(e):
                run("act", e)

            @block.vector
            def _(e):
                run("dve", e)

            @block.gpsimd
            def _(e):
                run("pool", e)

            @block.sync
            def _(e):
                run("sp", e)
        for e in ENGS:
            self.tdma[e] += self.ndma[e]
        self.ops = {e: [] for e in ENGS}
        self.ndma = {e: 0 for e in ENGS}
        self.dma_ops = {e: [] for e in ENGS}
        self.writers = {}
        self.readers = {}
        self.known = {e: {} for e in ENGS}
        self.kdma = {e: set() for e in ENGS}

from concourse.bass_utils import run_bass_kernel_spmd

D = 2048
KC = 16
EVIN = 6400
ODIN = 5120
ALPHA = 8 ** 0.25
CDEC = float(np.exp(-0.5))
NPAR = 276
NPQ = 136
POOLW = (2, 4, 8, 16)


def _blocks(L, T, bw=512):
    out = []
    for (s0, ln) in ((0, L), (L, T)):
        t = s0
        while t < s0 + ln:
            w = min(bw, s0 + ln - t)
            out.append((s0, ln, t, w))
            t += w
    return out


def build(T, L, NB, depth=4, upto=99):
    N = L + T
    NC3 = NB + 1
    NCH = N // 64
    LCH = L // 64
    nc = bass.Bass("TRN2", target_bir_lowering=False)
    din = lambda name, shape, dt=F32: nc.dram_tensor(name, list(shape), dt, kind="ExternalInput").ap()
    dsc = lambda name, shape, dt=F32: nc.dram_tensor(name, list(shape), dt).ap()
    xin = din("xin", [NB, N, D])
    cT = din("cT", [128, KC, NC3])
    mod_w = din("mod_w", [4, D, 3 * D])
    modbT = din("modbT", [128, 4, 48])
    ln_g = din("ln_g", [4, D])
    ln_b = din("ln_b", [4, D])
    ev_w_in = din("ev_w_in", [2, D, EVIN])
    ev_w_out = din("ev_w_out", [2, D, D])
    pool_w = din("pool_w", [2, 4, 256, 256])
    evp = din("evp", [64, 2, NPAR])
    evq = din("evq", [128, 2, NPQ])
    ps128 = din("ps128", [128, 2, 8])
    rw_w2 = din("rw_w2", [2, 2, 64, 1024])
    rw_a2 = din("rw_a2", [2, 2, 64, 1024])
    od_w_in = din("od_w_in", [2, D, ODIN])
    od_w_out = din("od_w_out", [2, D, D])
    qkg = din("qkg", [128, 2, 2])
    c_ident = din("c_ident", [128, 128])
    c_rt = din("c_rt", [128, 128])
    c_mk8 = din("c_mk8", [64, 8, 4, 64])
    c_mn8 = din("c_mn8", [64, 8, 64])
    c_cos = din("c_cos", [128, T])
    c_sin = din("c_sin", [128, T])
    c_rcnt = din("c_rcnt", [4, N])
    yout = nc.dram_tensor("yout", [NB, T, D], F32, kind="ExternalOutput").ap()
    XS = dsc("XS", [NB, N, D])
    PT = dsc("PT", [EVIN, N])
    QT = dsc("QT", [2048 + 512, N], BF16)
    VT = dsc("VT", [N, 512], BF16)
    MT = dsc("MT", [D, N], BF16)
    OPS = dsc("OPS", [2, 16, 5, 64, N])
    GAM = dsc("GAM", [2, 16, 64, NCH])
    BON = dsc("BON", [2, 16, 64, N])
    YT = dsc("YT", [2, 16, 64, N])

    S = Sched(nc)
    top = contextlib.ExitStack()
    S.begin(top)
    uid = [0]

    def sb(st, shape, dt=F32, name="t"):
        uid[0] += 1
        return st.enter_context(nc.sbuf_tensor(f"{name}{uid[0]}", list(shape), dt))

    def ps(st, shape, dt=F32, name="p"):
        uid[0] += 1
        return st.enter_context(nc.psum_tensor(f"{name}{uid[0]}", list(shape), dt))

    ukey = [0]

    def uk(tag):
        ukey[0] += 1
        return (tag, ukey[0])

    rr = [0]

    def EW():
        rr[0] += 1
        return "pool" if rr[0] % 3 == 0 else "dve"

    ident = sb(top, [128, 128], name="ident")
    onesf = sb(top, [128, 128], name="onesf")
    modT = sb(top, [128, depth, 48, NC3], name="modT")
    cst = sb(top, [128, 4], name="cst")
    blk1 = sb(top, [128, 128], name="blk1")

    with contextlib.ExitStack() as st:
        S.dma(ident[:], c_ident[:, :])
        S.memset("dve", onesf[:], 1.0)
        S.memset("dve", cst[:, 0:1], 1e-12)
        S.memset("pool", blk1[:], 0.0)
        S.memset("pool", blk1[0:64, 0:64], 1.0)
        S.memset("pool", blk1[64:128, 64:128], 1.0)
        S.memset("dve", cst[:, 1:2], 64e-5)
        S.memset("dve", cst[:, 2:3], 1e-6)
        ct = sb(st, [128, KC, NC3])
        sc = sb(st, [128, KC, NC3])
        mb = sb(st, [128, 4, 48])
        S.dma(ct[:], cT[:, :, :])
        S.dma(mb[:], modbT[:, :, :])
        S.act(sc[:], ct[:], AF.Silu)
        wsts = [sb(st, [128, KC, 512]) for _ in range(2)]
        pm = [ps(st, [128, 8]) for _ in range(2)]
        for l in range(depth):
            for cb in range(12):
                w = wsts[cb % 2]
                S.dma(w[:], mod_w[l, :, cb * 512:(cb + 1) * 512].rearrange("(k p) c -> p k c", p=128))
                for m in range(4):
                    j = cb * 4 + m
                    p = pm[j % 2]
                    for k in range(KC):
                        S.matmul(p[:, 0:NC3], lhsT=w[:, k, m * 128:(m + 1) * 128], rhs=sc[:, k, :],
                                 start=(k == 0), stop=(k == KC - 1))
                    S.act(modT[:, l, j, :], p[:, 0:NC3], AF.Identity, bias=mb[:, l, j:j + 1])
            S.ts("dve", modT[:, l, 16:32, :], modT[:, l, 16:32, :], 1.0, ALU.add)
        S.flush()

    def build_hT(st, b, l):
        src = xin if l == 0 else XS
        hT = [sb(st, [128, N], BF16, name="hT") for _ in range(KC)]
        xt = [sb(st, [128, D]) for _ in range(2)]
        pT = [ps(st, [128, 512]) for _ in range(2)]
        for tt in range(N // 128):
            col = NB if tt * 128 < L else b
            x = xt[tt % 2]
            S.dma(x[:], src[b, tt * 128:(tt + 1) * 128, :], rkeys=[("X", tt)])
            for k4 in range(4):
                p = pT[(tt * 4 + k4) % 2]
                for j in range(4):
                    k = k4 * 4 + j
                    S.transpose(p[:, j * 128:(j + 1) * 128], x[:, k * 128:(k + 1) * 128], ident[:])
                for j in range(4):
                    k = k4 * 4 + j
                    o = hT[k][:, tt * 128:(tt + 1) * 128]
                    if (tt * 4 + k4) % 2 == 0:
                        S.act(o, p[:, j * 128:(j + 1) * 128], AF.Identity,
                              scale=modT[:, l, 16 + k, col:col + 1], bias=modT[:, l, k, col:col + 1])
                    else:
                        S.ts("dve", o, p[:, j * 128:(j + 1) * 128], modT[:, l, 16 + k, col:col + 1], ALU.mult,
                             modT[:, l, k, col:col + 1], ALU.add)
        return hT

    tblocks = []
    t = 0
    while t < N:
        w = min(512, N - t)
        tblocks.append((t, w))
        t += w

    def phase_inproj_even(b, l):
        i = l // 2
        with contextlib.ExitStack() as st:
            hT = build_hT(st, b, l)
            wst = sb(st, [128, KC, 256])
            wbf = [sb(st, [128, KC, 256], BF16) for _ in range(2)]
            pp = [ps(st, [128, 512]) for _ in range(4)]
            stg = [sb(st, [128, 512]) for _ in range(4)]
            cnt = 0

            def loadw(cb):
                S.dma(wst[:], ev_w_in[i, :, cb * 256:(cb + 1) * 256].rearrange("(k p) c -> p k c", p=128))
                S.copy("pool", wbf[cb % 2][:], wst[:])

            loadw(0)
            for cb in range(EVIN // 256):
                if cb + 1 < EVIN // 256:
                    loadw(cb + 1)
                wb = wbf[cb % 2]
                for m in range(2):
                    for (t0, w) in tblocks:
                        p = pp[cnt % 4]
                        sg = stg[cnt % 4]
                        for k in range(KC):
                            S.matmul(p[:, 0:w], lhsT=wb[:, k, m * 128:(m + 1) * 128], rhs=hT[k][:, t0:t0 + w],
                                     start=(k == 0), stop=(k == KC - 1))
                        S.copy("act" if cnt % 2 == 0 else "dve", sg[:, 0:w], p[:, 0:w])
                        r0 = cb * 256 + m * 128
                        S.dma(PT[r0:r0 + 128, t0:t0 + w], sg[:, 0:w], wkeys=[uk("PT")])
                        cnt += 1
            S.flush()

    def phase_pool(b, l):
        i = l // 2
        last = l == depth - 1
        with contextlib.ExitStack() as st:
            psc = sb(st, [128, 8])
            S.dma(psc[:], ps128[:, i, :])
            pw = sb(st, [128, 2, 256])
            pwb = sb(st, [128, 2, 256], BF16)
            mk2 = lambda shape, dt=F32: [[sb(st, shape, dt) for _ in range(2)] for _ in range(2)]
            ex2, wk2 = mk2([128, 528]), mk2([128, 528])
            rc2 = [sb(st, [128, 512]) for _ in range(2)]
            pb2 = mk2([128, 512], BF16)
            gt2, sgt2 = mk2([128, 512]), mk2([128, 512])
            ob2 = mk2([128, 512], BF16)
            pq2 = [[ps(st, [128, 512]) for _ in range(2)] for _ in range(2)]
            itn = [0]
            for g in range(4):
                nadd = g + 1
                h = POOLW[g] // 2
                S.dma(pw[:], pool_w[i, g].rearrange("(c p) d -> p c d", p=128))
                S.copy("pool", pwb[:], pw[:])
                for (s0, ln, t0, w) in _blocks(L, T):
                    if last and s0 == 0:
                        continue
                    ip = itn[0] % 2
                    itn[0] += 1
                    ex, wk, rc, pb, gt, sgt, ob, pq = ex2[ip], wk2[ip], rc2[ip], pb2[ip], gt2[ip], sgt2[ip], ob2[ip], pq2[ip]
                    S.dma(rc[:, 0:w], c_rcnt[g:g + 1, t0:t0 + w].partition_broadcast(128))
                    for cc in range(2):
                        j = g * 2 + cc
                        e = ex[cc]
                        lo = max(s0, t0 - 8)
                        hi = min(s0 + ln, t0 + w + 8)
                        S.memset("pool", e[:, 0:w + 16], 0.0)
                        S.dma(e[:, 8 - (t0 - lo):8 - (t0 - lo) + (hi - lo)], PT[j * 128:(j + 1) * 128, lo:hi])
                        cur = e
                        width = w + 16
                        for a in range(nadd):
                            sh = 1 << a
                            nw = width - sh
                            dst = wk[a % 2]
                            S.tt("dve", dst[:, 0:nw], cur[:, 0:nw], cur[:, sh:sh + nw], ALU.add)
                            cur = dst
                            width = nw
                        tmp = wk[nadd % 2]
                        S.tt("dve", tmp[:, 0:w], cur[:, 8 - h:8 - h + w], rc[:, 0:w], ALU.mult)
                        S.tt("pool", pb[cc][:, 0:w], tmp[:, 0:w], e[:, 8:8 + w], ALU.subtract)
                    for dj in range(2):
                        jj = g * 2 + dj
                        p = pq[dj]
                        for cc in range(2):
                            S.matmul(p[:, 0:w], lhsT=pwb[:, cc, dj * 128:(dj + 1) * 128], rhs=pb[cc][:, 0:w],
                                     start=(cc == 0), stop=(cc == 1))
                        S.dma(gt[dj][:, 0:w], PT[4352 + jj * 128:4352 + (jj + 1) * 128, t0:t0 + w])
                        S.act(sgt[dj][:, 0:w], gt[dj][:, 0:w], AF.Silu)
                        S.stt(ob[dj][:, 0:w], p[:, 0:w], psc[:, jj:jj + 1], sgt[dj][:, 0:w], ALU.mult, ALU.mult)
                        S.dma(MT[jj * 128:(jj + 1) * 128, t0:t0 + w], ob[dj][:, 0:w], wkeys=[uk("MT")])
            S.flush()

    def phase_rwkv_prep(b, l):
        i = l // 2
        with contextlib.ExitStack() as st:
            par = sb(st, [64, NPAR])
            S.dma(par[:], evp[:, i, :])
            parq = sb(st, [128, NPQ])
            S.dma(parq[:], evq[:, i, :])
            w2t = [sb(st, [64, 1024]) for _ in range(2)]
            a2t = [sb(st, [64, 1024]) for _ in range(2)]
            for d in range(2):
                S.dma(w2t[d][:], rw_w2[i, d, :, :])
                S.dma(a2t[d][:], rw_a2[i, d, :, :])
            ones1 = sb(st, [128, 512])
            S.memset("dve", ones1[:], 1.0)
            T_ = lambda n=514: sb(st, [128, n])
            T64 = lambda n=514: sb(st, [64, n])
            wdx = [T64() for _ in range(2)]
            adx = [T64() for _ in range(2)]
            tw = [T64(512) for _ in range(2)]
            sa = [T64(512) for _ in range(2)]
            tl = [T64(512) for _ in range(2)]
            x3 = [[T_() for _ in range(3)] for _ in range(2)]
            tmp = [T_(512) for _ in range(4)]
            rkv = [[T_(512) for _ in range(3)] for _ in range(2)]
            sig = [T_(512) for _ in range(2)]
            asg = [T_(512) for _ in range(2)]
            kk = [T_(512) for _ in range(2)]
            sq = [T_(512) for _ in range(2)]
            rn = [T_(512) for _ in range(2)]
            kkn = [T_(512) for _ in range(2)]
            kmod = [T_(512) for _ in range(2)]
            bvec = [T_(512) for _ in range(2)]
            rk = [T_(512) for _ in range(2)]
            bon = [T_(512) for _ in range(2)]
            pe_ = [T_(580) for _ in range(2)]
            gs = [T_(512) for _ in range(2)]
            gm = [T_(512) for _ in range(2)]
            eG = [T_(512) for _ in range(2)]
            eGi = [T_(512) for _ in range(2)]
            eGm = [T_(512) for _ in range(2)]
            oq = [[T_(512) for _ in range(4)] for _ in range(2)]
            pw_ = [ps(st, [128, 512]) for _ in range(2)]
            pa_ = [ps(st, [128, 512]) for _ in range(2)]
            pss = [ps(st, [128, 512]) for _ in range(2)]
            pbn = [ps(st, [128, 512]) for _ in range(2)]
            for d in range(2):
                S.memset("dve", pe_[d][:, 0:1], 0.0)

            def load_halo(dst, row0, s0, ln, t0, w, nr=64):
                lo = max(s0, t0 - 1)
                hi = min(s0 + ln, t0 + w + 1)
                if lo > t0 - 1:
                    S.memset("pool", dst[:, 0:1], 0.0)
                if hi < t0 + w + 1:
                    S.memset("pool", dst[:, w + 1:w + 2], 0.0)
                o = 1 - (t0 - lo)
                S.dma(dst[:, o:o + hi - lo], PT[row0:row0 + nr, lo:hi])

            def shiftmix(out, X, d, w, mucol, scratch):
                nbv = X[:, 0:w] if d == 0 else X[:, 2:w + 2]
                cur = X[:, 1:w + 1]
                S.tt(EW(), scratch[:, 0:w], nbv, cur, ALU.subtract)
                S.stt(out, scratch[:, 0:w], mucol, cur, ALU.mult, ALU.add)

            it = 0
            ewc = [0]

            def EW():
                ewc[0] += 1
                return "dve" if ewc[0] % 10 in (0, 3, 7) else "pool"

            for (s0, ln, t0, w) in _blocks(L, T):
                nchb = w // 64
                c0 = t0 // 64
                for d in range(2):
                    load_halo(wdx[d], 4096 + 64 * d, s0, ln, t0, w)
                    load_halo(adx[d], 4224 + 64 * d, s0, ln, t0, w)
                    shiftmix(tl[0][:, 0:w], wdx[d], d, w, par[:, 272 + d * 2:273 + d * 2], tl[1])
                    S.act(tw[d][:, 0:w], tl[0][:, 0:w], AF.Tanh)
                    shiftmix(sa[d][:, 0:w], adx[d], d, w, par[:, 273 + d * 2:274 + d * 2], tl[1])
                for h in range(8):
                    xs_ = x3[h % 2]
                    for q in range(3):
                        load_halo(xs_[q], 1024 * (q + 1) + 128 * h, s0, ln, t0, w, nr=128)
                    for d in range(2):
                        z = it % 2
                        it += 1
                        r, k, v = (rkv[z][q][:, 0:w] for q in range(3))
                        for q, o in enumerate((r, k, v)):
                            c = (d * 3 + q) * 8 + h
                            shiftmix(o, xs_[q], d, w, parq[:, c:c + 1], tmp[2 + (q % 2)])
                        S.matmul(pw_[z][:, 0:w], lhsT=w2t[d][:, 128 * h:128 * h + 128], rhs=tw[d][:, 0:w])
                        S.act(sig[z][:, 0:w], pw_[z][:, 0:w], AF.Sigmoid, bias=parq[:, 48 + d * 8 + h:49 + d * 8 + h])
                        S.matmul(pa_[z][:, 0:w], lhsT=a2t[d][:, 128 * h:128 * h + 128], rhs=sa[d][:, 0:w])
                        S.act(asg[z][:, 0:w], pa_[z][:, 0:w], AF.Sigmoid, bias=parq[:, 64 + d * 8 + h:65 + d * 8 + h])
                        S.act(kk[z][:, 0:w], k, AF.Identity, scale=parq[:, 80 + h:81 + h])
                        S.act(sq[z][:, 0:w], k, AF.Square, scale=parq[:, 80 + h:81 + h])
                        S.matmul(pss[z][:, 0:w], lhsT=blk1[:], rhs=sq[z][:, 0:w])
                        S.act(rn[z][:, 0:w], pss[z][:, 0:w], AF.Sqrt, bias=cst[:, 0:1])
                        S.op("dve", "reciprocal", [rn[z][:]], [rn[z][:]], rn[z][:, 0:w], rn[z][:, 0:w])
                        S.tt(EW(), kkn[z][:, 0:w], kk[z][:, 0:w], rn[z][:, 0:w], ALU.mult)
                        S.ts(EW(), tmp[0][:, 0:w], asg[z][:, 0:w], -1.0, ALU.add, parq[:, 88 + h:89 + h], ALU.mult)
                        S.stt(kmod[z][:, 0:w], tmp[0][:, 0:w], 1.0, k, ALU.add, ALU.mult)
                        S.tt(EW(), bvec[z][:, 0:w], kkn[z][:, 0:w], asg[z][:, 0:w], ALU.mult)
                        S.stt(rk[z][:, 0:w], r, parq[:, 96 + h:97 + h], kmod[z][:, 0:w], ALU.mult, ALU.mult)
                        S.matmul(pbn[z][:, 0:w], lhsT=blk1[:], rhs=rk[z][:, 0:w])
                        S.tt("dve", bon[z][:, 0:w], pbn[z][:, 0:w], v, ALU.mult)
                        S.dma(BON[d].rearrange("h p n -> (h p) n")[128 * h:128 * h + 128, t0:t0 + w], bon[z][:, 0:w], wkeys=[uk("BON")])
                        P = pe_[z]
                        S.op("dve", "tensor_tensor_scan", [ones1[:], sig[z][:]], [P[:]], P[:, 1:w + 1], ones1[:, 0:w],
                             sig[z][:, 0:w], 0.0, ALU.mult, ALU.add)
                        g3 = gs[z][:, 0:w].rearrange("p (c j) -> p c j", j=64)
                        if d == 0:
                            a0_ = P[:, 1:w + 1].rearrange("p (c j) -> p c j", j=64)
                            b0_ = P[:, 0:w].rearrange("p (c j) -> p c j", j=64)[:, :, 0:1].broadcast_to([128, nchb, 64])
                            S.tt("dve", g3, a0_, b0_, ALU.subtract)
                        else:
                            a0_ = P[:, 64:w + 64].rearrange("p (c j) -> p c j", j=64)[:, :, 0:1].broadcast_to([128, nchb, 64])
                            b0_ = P[:, 0:w].rearrange("p (c j) -> p c j", j=64)
                            S.tt("dve", g3, a0_, b0_, ALU.subtract)
                        S.tt(EW(), gm[z][:, 0:w], gs[z][:, 0:w], sig[z][:, 0:w], ALU.subtract)
                        S.act(eG[z][:, 0:w], gs[z][:, 0:w], AF.Exp, scale=-CDEC)
                        S.act(eGi[z][:, 0:w], gs[z][:, 0:w], AF.Exp, scale=CDEC)
                        S.act(eGm[z][:, 0:w], gm[z][:, 0:w], AF.Exp, scale=-CDEC)
                        o4 = oq[z]
                        S.stt(o4[0][:, 0:w], kkn[z][:, 0:w], -1.0, eGm[z][:, 0:w], ALU.mult, ALU.mult)
                        S.tt(EW(), o4[1][:, 0:w], r, eG[z][:, 0:w], ALU.mult)
                        S.tt(EW(), o4[2][:, 0:w], bvec[z][:, 0:w], eGi[z][:, 0:w], ALU.mult)
                        S.tt(EW(), o4[3][:, 0:w], kmod[z][:, 0:w], eGi[z][:, 0:w], ALU.mult)
                        for q in range(4):
                            for hh in range(2):
                                S.dma(OPS[d, 2 * h + hh, q, :, t0:t0 + w], o4[q][64 * hh:64 * hh + 64, 0:w], wkeys=[uk("OPS")])
                        for hh in range(2):
                            S.dma(OPS[d, 2 * h + hh, 4, :, t0:t0 + w], rkv[z][2][64 * hh:64 * hh + 64, 0:w], wkeys=[uk("OPS")])
                        e3 = eG[z][:, 0:w].rearrange("p (c j) -> p c j", j=64)
                        gsrc = e3[:, :, 63] if d == 0 else e3[:, :, 0]
                        S.copy("pool", tmp[1][:, 0:nchb], gsrc)
                        S.dma(GAM[d].rearrange("h p c -> (h p) c")[128 * h:128 * h + 128, c0:c0 + nchb], tmp[1][:, 0:nchb], wkeys=[uk("GAM")])
            S.flush()

    def phase_rwkv_scan(b, l):
        order = [list(range(NCH)), list(range(LCH - 1, -1, -1)) + list(range(NCH - 1, LCH - 1, -1))]
        with contextlib.ExitStack() as st:
            mk8 = sb(st, [64, 8, 4, 64])
            mn8 = sb(st, [64, 8, 64])
            S.dma(mk8[:], c_mk8[:, :, :, :])
            S.dma(mn8[:], c_mn8[:, :, :])
            OPb = [[sb(st, [64, 4, 5, 64]) for _ in range(2)] for _ in range(3)]
            OPr = [[sb(st, [64, 4, 4, 64], F32R) for _ in range(2)] for _ in range(3)]
            gam = [sb(st, [64, 4, NCH]) for _ in range(2)]
            TM2 = [sb(st, [64, 8, 3, 64], F32R) for _ in range(2)]
            AM2 = [sb(st, [64, 8, 4, 64], F32R) for _ in range(2)]
            Xs2 = [[None] + [sb(st, [64, 8, 64], F32R) for _ in range(5)] for _ in range(2)]
            Ys2 = [[sb(st, [64, 8, 64], F32R) for _ in range(5)] for _ in range(2)]
            U = sb(st, [64, 8, 64], F32R)
            Hr = sb(st, [64, 8, 64], F32R)
            H1 = sb(st, [64, 8, 64])
            ys = [sb(st, [64, 8, 64]) for _ in range(2)]
            big = ps(st, [64, 2048])
            PX = ps(st, [64, 512])
            PY = ps(st, [64, 512])
            pa = ps(st, [64, 512])
            pb_ = ps(st, [64, 512])
            A1 = big[:].rearrange("p (v c) -> p v c", c=256)
            TMp = big[:, 0:1536].rearrange("p (v q c) -> p v q c", q=3, c=64)
            v3 = lambda t_: t_[:].rearrange("p (v c) -> p v c", c=64)
            mkf = mk8[:].rearrange("p v q c -> p (v q c)")
            for g4 in range(4):
                h0 = g4 * 4
                for d in range(2):
                    S.dma(gam[d][:], GAM[d, h0:h0 + 4].rearrange("h p c -> p h c"), rkeys=[])
                S.memset("dve", H1[:], 0.0)
                S.copy("pool", Hr[:], H1[:])

                def load(s_):
                    o3 = s_ % 3
                    for d in range(2):
                        c = order[d][s_]
                        S.dma(OPb[o3][d][:],
                              OPS[d, h0:h0 + 4, :, :, c * 64:(c + 1) * 64].rearrange("h q p c -> p h q c"), rkeys=[])
                        S.copy("pool", OPr[o3][d][:], OPb[o3][d][:, :, 0:4, :])

                def Xl(s_):
                    p2 = s_ % 2
                    return [AM2[p2][:, :, 0, :]] + [x_[:] for x_ in Xs2[p2][1:]]

                def a_stages(s_):
                    p2, o3 = s_ % 2, s_ % 3
                    TM, AM, Xs, Ys = TM2[p2], AM2[p2], Xs2[p2], Ys2[p2]
                    O32 = lambda vh, q: OPb[o3][vh // 4][:, vh % 4, q, :]
                    O = lambda vh, q: OPr[o3][vh // 4][:, vh % 4, q, :]
                    X = Xl(s_)

                    def a0():
                        for vh in range(8):
                            for qi in range(3):
                                S.transpose(TMp[:, vh, qi, :], O32(vh, 2 + qi), ident[0:64, 0:64])
                        TMf = TM[:].rearrange("p v q c -> p (v q c)")
                        for q3 in range(3):
                            S.copy("act", TMf[:, q3 * 512:(q3 + 1) * 512], big[:, q3 * 512:(q3 + 1) * 512])

                    def a1():
                        for vh in range(8):
                            ar = OPr[o3][vh // 4][:, vh % 4, 0:2, :].rearrange("p a c -> p (a c)")
                            S.matmul(A1[:, vh, 0:128], lhsT=O(vh, 2), rhs=ar)
                            S.matmul(A1[:, vh, 128:256], lhsT=O(vh, 3), rhs=ar)
                            S.matmul(v3(PX)[:, vh, :], lhsT=O(vh, 0), rhs=O(vh, 2))
                        AMf = AM[:].rearrange("p v q c -> p (v q c)")
                        for q4 in range(4):
                            S.tt("dve", AMf[:, q4 * 512:(q4 + 1) * 512], big[:, q4 * 512:(q4 + 1) * 512],
                                 mkf[:, q4 * 512:(q4 + 1) * 512], ALU.mult)
                        S.tt("dve", Ys[0][:], v3(PX), mn8[:], ALU.mult)

                    def asq(i_):
                        def f():
                            for vh in range(8):
                                S.matmul(v3(PX)[:, vh, :], lhsT=Ys[i_ - 1][:, vh, :], rhs=X[i_ - 1][:, vh, :])
                            if i_ < 5:
                                for vh in range(8):
                                    S.matmul(v3(PY)[:, vh, :], lhsT=X[i_ - 1][:, vh, :], rhs=Ys[i_ - 1][:, vh, :])
                            S.copy("act", Xs[i_][:], v3(PX))
                            if i_ < 5:
                                S.copy("dve", Ys[i_][:], v3(PY))
                        return f

                    return [a0, a1] + [asq(i_) for i_ in range(1, 6)]

                def b_stages(s_):
                    p2, o3 = s_ % 2, s_ % 3
                    TM, AM = TM2[p2], AM2[p2]
                    O = lambda vh, q: OPr[o3][vh // 4][:, vh % 4, q, :]
                    X = Xl(s_)
                    cs = [order[0][s_], order[1][s_]]

                    def b0():
                        for vh in range(8):
                            S.matmul(v3(pa)[:, vh, :], lhsT=O(vh, 0), rhs=Hr[:, vh, :], start=True, stop=False)
                            S.matmul(v3(pa)[:, vh, :], lhsT=AM[:, vh, 2, :], rhs=TM[:, vh, 2, :], start=False, stop=True)
                        S.copy("dve", U[:], v3(pa))

                    def bap(i_):
                        def f():
                            for vh in range(8):
                                S.matmul(v3(pb_)[:, vh, :], lhsT=X[i_][:, vh, :], rhs=U[:, vh, :])
                            S.tt("dve", U[:], U[:].bitcast(F32), v3(pb_), ALU.add)
                        return f

                    def b7():
                        for vh in range(8):
                            o = v3(pa)[:, vh, :]
                            S.matmul(o, lhsT=Hr[:, vh, :], rhs=O(vh, 1), start=True, stop=False)
                            S.matmul(o, lhsT=U[:, vh, :], rhs=AM[:, vh, 1, :], start=False, stop=False)
                            S.matmul(o, lhsT=TM[:, vh, 2, :], rhs=AM[:, vh, 3, :], start=False, stop=True)
                        y_ = ys[s_ % 2]
                        S.copy("act", y_[:], v3(pa))
                        for d in range(2):
                            c = cs[d]
                            S.dma(YT[d, h0:h0 + 4, :, c * 64:(c + 1) * 64].rearrange("h p c -> p h c"),
                                  y_[:, d * 4:(d + 1) * 4, :], wkeys=[uk("YT")])

                    def b8():
                        for vh in range(8):
                            o = v3(pb_)[:, vh, :]
                            S.matmul(o, lhsT=TM[:, vh, 0, :], rhs=U[:, vh, :], start=True, stop=False)
                            S.matmul(o, lhsT=TM[:, vh, 1, :], rhs=TM[:, vh, 2, :], start=False, stop=True)
                        S.tt("dve", H1[:], Hr[:].bitcast(F32), v3(pb_), ALU.add)
                        for d in range(2):
                            c = cs[d]
                            S.tt("dve", Hr[:, d * 4:(d + 1) * 4, :], H1[:, d * 4:(d + 1) * 4, :],
                                 gam[d][:, :, c:c + 1].broadcast_to([64, 4, 64]), ALU.mult)

                    return [b0] + [bap(i_) for i_ in range(6)] + [b7, b8]

                load(0)
                if NCH > 1:
                    load(1)
                for f in a_stages(0):
                    f()
                for s_ in range(NCH):
                    if s_ + 2 < NCH:
                        load(s_ + 2)
                    Bs = b_stages(s_)
                    As = a_stages(s_ + 1) if s_ + 1 < NCH else []
                    for i_ in range(max(len(Bs), len(As))):
                        if i_ < len(Bs):
                            Bs[i_]()
                        if i_ < len(As):
                            As[i_]()
            S.flush()

    def phase_rwkv_out(b, l):
        i = l // 2
        last = l == depth - 1
        with contextlib.ExitStack() as st:
            par = sb(st, [128, NPQ])
            S.dma(par[:], evq[:, i, :])
            yb2 = [[sb(st, [128, 512]) for _ in range(2)] for _ in range(2)]
            bb2 = [[sb(st, [128, 512]) for _ in range(2)] for _ in range(2)]
            yc2 = [[sb(st, [128, 512]) for _ in range(2)] for _ in range(2)]
            sq2 = [[sb(st, [128, 512]) for _ in range(2)] for _ in range(2)]
            rs2 = [[sb(st, [128, 512]) for _ in range(2)] for _ in range(2)]
            acc = [sb(st, [128, 512]) for _ in range(2)]
            gt = [sb(st, [128, 512]) for _ in range(2)]
            sg = [sb(st, [128, 512]) for _ in range(2)]
            ob = [sb(st, [128, 512], BF16) for _ in range(2)]
            pm2 = [[ps(st, [128, 512]) for _ in range(2)] for _ in range(2)]
            pv2 = [[ps(st, [128, 512]) for _ in range(2)] for _ in range(2)]
            it = 0
            for h in range(8):
                for (s0, ln, t0, w) in _blocks(L, T):
                    if last and s0 == 0:
                        continue
                    zz = it % 2
                    it += 1
                    yb, bb, yc, sq, rs, pm, pv = yb2[zz], bb2[zz], yc2[zz], sq2[zz], rs2[zz], pm2[zz], pv2[zz]
                    R2 = range(2)
                    for d in R2:
                        S.dma(yb[d][:, 0:w], YT[d].rearrange("h p n -> (h p) n")[128 * h:128 * h + 128, t0:t0 + w])
                        S.dma(bb[d][:, 0:w], BON[d].rearrange("h p n -> (h p) n")[128 * h:128 * h + 128, t0:t0 + w])
                    for d in R2:
                        S.matmul(pm[d][:, 0:w], lhsT=blk1[:], rhs=yb[d][:, 0:w])
                    for d in R2:
                        S.stt(yc[d][:, 0:w], pm[d][:, 0:w], -1.0 / 64, yb[d][:, 0:w], ALU.mult, ALU.add)
                    for d in R2:
                        S.tt("pool" if d else "dve", sq[d][:, 0:w], yc[d][:, 0:w], yc[d][:, 0:w], ALU.mult)
                    for d in R2:
                        S.matmul(pv[d][:, 0:w], lhsT=blk1[:], rhs=sq[d][:, 0:w])
                    for d in R2:
                        S.act(rs[d][:, 0:w], pv[d][:, 0:w], AF.Sqrt, bias=cst[:, 1:2], scale=1.0 / 64)
                    for d in R2:
                        S.op("dve", "reciprocal", [rs[d][:]], [rs[d][:]], rs[d][:, 0:w], rs[d][:, 0:w])
                    for d in R2:
                        S.tt("pool" if d else "dve", yc[d][:, 0:w], yc[d][:, 0:w], rs[d][:, 0:w], ALU.mult)
                    for d in R2:
                        S.ts("pool" if d else "dve", yc[d][:, 0:w], yc[d][:, 0:w],
                             par[:, 104 + d * 8 + h:105 + d * 8 + h], ALU.mult,
                             par[:, 120 + d * 8 + h:121 + d * 8 + h], ALU.add)
                    for d in R2:
                        S.tt("pool" if d else "dve", yc[d][:, 0:w], yc[d][:, 0:w], bb[d][:, 0:w], ALU.add)
                    S.tt(EW(), acc[zz][:, 0:w], yc[0][:, 0:w], yc[1][:, 0:w], ALU.add)
                    r0 = 4352 + 1024 + 128 * h
                    S.dma(gt[zz][:, 0:w], PT[r0:r0 + 128, t0:t0 + w])
                    S.act(sg[zz][:, 0:w], gt[zz][:, 0:w], AF.Silu)
                    S.tt(EW(), ob[zz][:, 0:w], acc[zz][:, 0:w], sg[zz][:, 0:w], ALU.mult)
                    S.dma(MT[1024 + 128 * h:1024 + 128 * h + 128, t0:t0 + w], ob[zz][:, 0:w], wkeys=[uk("MT")])
            S.flush()

    def phase_inproj_odd(b, l):
        i = l // 2
        with contextlib.ExitStack() as st:
            hT = build_hT(st, b, l)
            wst = sb(st, [128, KC, 256])
            wbf = [sb(st, [128, KC, 256], BF16) for _ in range(2)]
            pp = [ps(st, [128, 512]) for _ in range(3)]
            pss = ps(st, [128, 512])
            prx = ps(st, [128, 512])
            stg = [sb(st, [128, 512]) for _ in range(2)]
            rt = sb(st, [128, 128])
            gq = sb(st, [128, 2])
            cosT = sb(st, [128, T])
            sinT = sb(st, [128, T])
            S.dma(rt[:], c_rt[:, :])
            S.dma(gq[:], qkg[:, i, :])
            S.dma(cosT[:], c_cos[:, :])
            S.dma(sinT[:], c_sin[:, :])
            xg = [sb(st, [128, 512], F32R) for _ in range(2)]
            sq = [sb(st, [128, 512], F32R) for _ in range(2)]
            rtr = sb(st, [128, 128], F32R)
            onr = sb(st, [128, 128], F32R)
            S.copy("dve", rtr[:], rt[:])
            S.copy("dve", onr[:], onesf[:])
            rs = [sb(st, [128, 512]) for _ in range(2)]
            t1 = [sb(st, [128, 512]) for _ in range(2)]
            t2 = [sb(st, [128, 512]) for _ in range(2)]
            ob = [sb(st, [128, 512], BF16) for _ in range(2)]
            vb = [sb(st, [128, 256], BF16) for _ in range(2)]
            cnt = 0

            def loadw(cb):
                S.dma(wst[:], od_w_in[i, :, cb * 256:(cb + 1) * 256].rearrange("(k p) c -> p k c", p=128))
                S.copy("pool", wbf[cb % 2][:], wst[:])

            loadw(0)
            for cb in range(ODIN // 256):
                if cb + 1 < ODIN // 256:
                    loadw(cb + 1)
                wb = wbf[cb % 2]
                c0 = cb * 256
                if 2560 <= c0 < 3072:
                    for tt in range(N // 128):
                        p = pp[cnt % 3]
                        for k in range(KC):
                            S.matmul(p[:, 0:256], lhsT=hT[k][:, tt * 128:(tt + 1) * 128], rhs=wb[:, k, :],
                                     start=(k == 0), stop=(k == KC - 1))
                        v_ = vb[cnt % 2]
                        S.copy("act" if cnt % 2 == 0 else "dve", v_[:], p[:, 0:256])
                        S.dma(VT[tt * 128:(tt + 1) * 128, c0 - 2560:c0 - 2560 + 256], v_[:], wkeys=[uk("VT")])
                        cnt += 1
                    continue
                for m in range(2):
                    col = c0 + m * 128
                    for (s0, ln, t0, w) in _blocks(L, T):
                        p = pp[cnt % 3]
                        z = cnt % 2
                        cnt += 1
                        for k in range(KC):
                            S.matmul(p[:, 0:w], lhsT=wb[:, k, m * 128:(m + 1) * 128], rhs=hT[k][:, t0:t0 + w],
                                     start=(k == 0), stop=(k == KC - 1))
                        if col >= 3072:
                            S.copy("act" if z == 0 else "dve", stg[z][:, 0:w], p[:, 0:w])
                            S.dma(PT[col - 3072:col - 3072 + 128, t0:t0 + w], stg[z][:, 0:w], wkeys=[uk("PT")])
                            continue
                        gcol = gq[:, 0:1] if col < 2048 else gq[:, 1:2]
                        S.act(xg[z][:, 0:w], p[:, 0:w], AF.Identity, scale=gcol)
                        S.act(sq[z][:, 0:w], p[:, 0:w], AF.Square)
                        S.matmul(pss[:, 0:w], lhsT=onr[:], rhs=sq[z][:, 0:w])
                        S.act(rs[z][:, 0:w], pss[:, 0:w], AF.Sqrt, bias=cst[:, 2:3], scale=1.0 / 128)
                        S.op("dve", "reciprocal", [rs[z][:]], [rs[z][:]], rs[z][:, 0:w], rs[z][:, 0:w])
                        if s0 == 0:
                            S.tt("dve", ob[z][:, 0:w], xg[z][:, 0:w].bitcast(F32), rs[z][:, 0:w], ALU.mult)
                        else:
                            S.matmul(prx[:, 0:w], lhsT=rtr[:], rhs=xg[z][:, 0:w])
                            S.tt("pool", t1[z][:, 0:w], xg[z][:, 0:w].bitcast(F32), cosT[:, t0 - L:t0 - L + w], ALU.mult)
                            S.tt("dve", t2[z][:, 0:w], prx[:, 0:w], sinT[:, t0 - L:t0 - L + w], ALU.mult)
                            S.tt("pool", t1[z][:, 0:w], t1[z][:, 0:w], t2[z][:, 0:w], ALU.add)
                            S.tt("dve", ob[z][:, 0:w], t1[z][:, 0:w], rs[z][:, 0:w], ALU.mult)
                        S.dma(QT[col:col + 128, t0:t0 + w], ob[z][:, 0:w], wkeys=[uk("QT")])
            S.flush()

    def phase_attn(b, l):
        last = l == depth - 1
        NKT = N // 128
        with contextlib.ExitStack() as st:
            kT = sb(st, [128, 4, N], BF16)
            V = sb(st, [128, NKT, 512], BF16)
            onesb = sb(st, [128, 128], BF16)
            S.dma(kT[:], QT[2048:2560, :].rearrange("(h p) t -> p h t", p=128))
            S.dma(V[:], VT[:, :].rearrange("(t p) c -> p t c", p=128))
            S.copy("dve", onesb[:], onesf[:])
            qT = [sb(st, [128, N], BF16) for _ in range(2)]
            gt = [sb(st, [128, N]) for _ in range(2)]
            psS = [ps(st, [128, 512]) for _ in range(3)]
            pex = [sb(st, [128, 512], BF16) for _ in range(3)]
            pO = [ps(st, [128, 512]) for _ in range(2)]
            pS = [ps(st, [128, 512]) for _ in range(2)]
            rsum = [sb(st, [128, 512]) for _ in range(2)]
            o1 = [sb(st, [128, 512]) for _ in range(2)]
            ob = [sb(st, [128, 512], BF16) for _ in range(2)]
            qblocks = []
            for (s0, ln, t0, w) in _blocks(L, T):
                if s0 == 0:
                    if not last:
                        qblocks.append((t0, w, list(range(L // 128))))
                else:
                    qblocks.append((t0, w, list(range(NKT))))
            items = []
            for h in range(16):
                for bi, (t0, w, kts) in enumerate(qblocks):
                    for ki, kt in enumerate(kts):
                        items.append((h, bi, ki))

            def issue_s(idx):
                h, bi, ki = items[idx]
                t0, w, kts = qblocks[bi]
                q_ = qT[h % 2]
                if bi == 0 and ki == 0:
                    g_ = gt[h % 2]
                    S.dma(q_[:], QT[h * 128:(h + 1) * 128, :])
                    S.dma(g_[:], PT[h * 128:(h + 1) * 128, :])
                    S.act(g_[:], g_[:], AF.Silu)
                kt = kts[ki]
                S.matmul(psS[idx % 3][:, 0:w], lhsT=kT[:, h // 4, kt * 128:(kt + 1) * 128], rhs=q_[:, t0:t0 + w])

            nblk = 0
            for idx in range(len(items)):
                if idx == 0:
                    issue_s(0)
                    if len(items) > 1:
                        issue_s(1)
                if idx + 2 < len(items):
                    issue_s(idx + 2)
                h, bi, ki = items[idx]
                kh = h // 4
                t0, w, kts = qblocks[bi]
                kt = kts[ki]
                z = nblk % 2
                pe2 = pex[idx % 3]
                S.act(pe2[:, 0:w], psS[idx % 3][:, 0:w], AF.Exp, scale=128 ** -0.5)
                S.matmul(pO[z][:, 0:w], lhsT=V[:, kt, kh * 128:(kh + 1) * 128], rhs=pe2[:, 0:w],
                         start=(ki == 0), stop=(ki == len(kts) - 1))
                S.matmul(pS[z][:, 0:w], lhsT=onesb[:], rhs=pe2[:, 0:w],
                         start=(ki == 0), stop=(ki == len(kts) - 1))
                if ki == len(kts) - 1:
                    g_ = gt[h % 2]
                    S.op("dve", "reciprocal", [pS[z][:]], [rsum[z][:]], rsum[z][:, 0:w], pS[z][:, 0:w])
                    S.tt("dve", o1[z][:, 0:w], pO[z][:, 0:w], rsum[z][:, 0:w], ALU.mult)
                    S.tt("pool", ob[z][:, 0:w], o1[z][:, 0:w], g_[:, t0:t0 + w], ALU.mult)
                    S.dma(MT[h * 128:(h + 1) * 128, t0:t0 + w], ob[z][:, 0:w], wkeys=[uk("MT")])
                    nblk += 1
            S.flush()

    def phase_out(b, l):
        i = l // 2
        last = l == depth - 1
        wout = ev_w_out if l % 2 == 0 else od_w_out
        src = xin if l == 0 else XS
        with contextlib.ExitStack() as st:
            gb = sb(st, [128, D])
            bbt = sb(st, [128, D])
            S.dma(gb[:], ln_g[l:l + 1, :].partition_broadcast(128))
            S.dma(bbt[:], ln_b[l:l + 1, :].partition_broadcast(128))
            xz2 = [sb(st, [128, 2, D]) for _ in range(2)]
            mt2 = [sb(st, [128, KC, 256], BF16) for _ in range(2)]
            wst2 = [sb(st, [128, KC, 256]) for _ in range(2)]
            walls = [sb(st, [128, KC, 256], BF16) for _ in range(D // 256)]
            for cb in range(D // 256):
                S.dma(wst2[cb % 2][:], wout[i, :, cb * 256:(cb + 1) * 256].rearrange("(k p) c -> p k c", p=128))
                S.copy("pool", walls[cb][:], wst2[cb % 2][:])
            pp = [ps(st, [128, 512]) for _ in range(2)]
            ptr = [ps(st, [128, 512]) for _ in range(2)]
            yg = [sb(st, [128, 256]) for _ in range(2)]
            junk = sb(st, [128, D], BF16)
            st2 = [sb(st, [128, 8]) for _ in range(2)]
            oo = [sb(st, [128, D]) for _ in range(2)]
            cnt = 0
            blks = [(s0, ln, t0, w) for (s0, ln, t0, w) in _blocks(L, T, bw=256) if not (last and s0 == 0)]

            def layer_norm_block(bi):
                s0, ln, t0, w = blks[bi]
                xz = xz2[bi % 2]
                for q in range(w // 128):
                    tt = t0 // 128 + q
                    zq = xz[:, q, :]
                    o_ = oo[q % 2]
                    st1 = st2[q % 2]
                    S.act(junk[:], zq, AF.Identity, accum_out=st1[:, 0:1])
                    S.ts("dve", st1[:, 1:2], st1[:, 0:1], -1.0 / D, ALU.mult)
                    S.act(junk[:], zq, AF.Square, bias=st1[:, 1:2], accum_out=st1[:, 2:3])
                    S.act(st1[:, 3:4], st1[:, 2:3], AF.Sqrt, bias=cst[:, 2:3], scale=1.0 / D)
                    S.op("dve", "reciprocal", [st1[:]], [st1[:]], st1[:, 4:5], st1[:, 3:4])
                    S.ts("dve", o_[:], zq, st1[:, 1:2], ALU.add, st1[:, 4:5], ALU.mult)
                    S.tt("pool", o_[:], o_[:], gb[:], ALU.mult)
                    S.tt("pool", o_[:], o_[:], bbt[:], ALU.add)
                    if last:
                        S.dma(yout[b, tt * 128 - L:(tt + 1) * 128 - L, :], o_[:], wkeys=[uk("Y")])
                    else:
                        S.dma(XS[b, tt * 128:(tt + 1) * 128, :], o_[:], wkeys=[("X", tt)])

            for bi, (s0, ln, t0, w) in enumerate(blks):
                nt = w // 128
                col = NB if s0 == 0 else b
                xz = xz2[bi % 2]
                mt = mt2[bi % 2]
                for q in range(nt):
                    tt = t0 // 128 + q
                    S.dma(xz[:, q, :], src[b, tt * 128:(tt + 1) * 128, :], rkeys=[("X", tt)], wkeys=[xz[:], ("xz", bi % 2, q)])
                S.dma(mt[:, :, 0:w], MT[:, t0:t0 + w].rearrange("(k p) t -> p k t", p=128))
                for dj in range(D // 128):
                    if dj == 3 and bi > 0:
                        layer_norm_block(bi - 1)
                    z = cnt % 2
                    cnt += 1
                    p = pp[z]
                    for k in range(KC):
                        S.matmul(p[:, 0:w], lhsT=walls[dj // 2][:, k, (dj % 2) * 128:(dj % 2) * 128 + 128], rhs=mt[:, k, 0:w],
                                 start=(k == 0), stop=(k == KC - 1))
                    S.act(yg[z][:, 0:w], p[:, 0:w], AF.Identity, scale=modT[:, l, 32 + dj, col:col + 1])
                    for q in range(nt):
                        S.transpose(ptr[z][:, q * 128:(q + 1) * 128], yg[z][:, q * 128:(q + 1) * 128], ident[:])
                    S.stt(xz[:, 0:nt, dj * 128:(dj + 1) * 128], xz[:, 0:nt, dj * 128:(dj + 1) * 128], ALPHA,
                          ptr[z][:, 0:w].rearrange("p (q c) -> p q c", c=128), ALU.mult, ALU.add)
            layer_norm_block(len(blks) - 1)
            S.flush()

    nph = [0]

    def run(f, b, l):
        if nph[0] < upto:
            f(b, l)
        nph[0] += 1

    for b in range(NB):
        for l in range(depth):
            if l % 2 == 0:
                run(phase_inproj_even, b, l)
                run(phase_pool, b, l)
                run(phase_rwkv_prep, b, l)
                run(phase_rwkv_scan, b, l)
                run(phase_rwkv_out, b, l)
            else:
                run(phase_inproj_odd, b, l)
                run(phase_attn, b, l)
            run(phase_out, b, l)
    top.close()
    return nc


def host_consts(T, L):
    N = L + T
    f = np.float32
    c = {}
    c["c_ident"] = np.eye(128, dtype=f)
    rt = np.zeros((128, 128), f)
    for m in range(128):
        half = (m % 64) // 32
        if half == 0:
            rt[m + 32, m] = -1.0
        else:
            rt[m - 32, m] = 1.0
    c["c_rt"] = rt
    s = np.arange(64)[:, None]
    t = np.arange(64)[None, :]
    mk8 = np.zeros((64, 8, 4, 64), f)
    mn8 = np.zeros((64, 8, 64), f)
    for vh in range(8):
        fwd = vh < 4
        strict = (s < t) if fwd else (s > t)
        incl = (s <= t) if fwd else (s >= t)
        mk8[:, vh, 0] = strict
        mk8[:, vh, 1] = incl
        mk8[:, vh, 2] = strict
        mk8[:, vh, 3] = incl
        mn8[:, vh] = (t < s) if fwd else (t > s)
    c["c_mk8"] = mk8
    c["c_mn8"] = mn8
    rows = np.repeat(np.arange(T // 64, dtype=f), 64)
    cols = np.tile(np.arange(64, dtype=f), T // 64)
    inv = (10000.0 ** (-np.arange(0, 64, 2, dtype=f) / 64)).astype(f)
    p = np.arange(128)
    axis = p // 64
    fr = p % 32
    pos = np.where(axis[:, None] == 0, rows[None, :], cols[None, :]).astype(f)
    ang = (pos * inv[fr][:, None]).astype(f)
    c["c_cos"] = np.cos(ang).astype(f)
    c["c_sin"] = np.sin(ang).astype(f)
    rc = np.zeros((4, N), f)
    for g, w in enumerate(POOLW):
        for (s0, ln) in ((0, L), (L, T)):
            tt = np.arange(ln)
            lo = np.clip(tt - w // 2, 0, ln)
            hi = np.clip(tt - w // 2 + w, 0, ln)
            rc[g, s0:s0 + ln] = 1.0 / (hi - lo).astype(f)
    c["c_rcnt"] = rc
    return c


def host_params(p):
    f = np.float32
    out = {}
    evp = np.zeros((64, 2, NPAR), f)
    for i in range(2):
        mu = p["rw_mu_rkv"][i].reshape(2, 3, 16, 64)
        for d in range(2):
            for q in range(3):
                evp[:, i, (d * 3 + q) * 16:(d * 3 + q) * 16 + 16] = mu[d, q].T
            evp[:, i, 96 + d * 16:96 + d * 16 + 16] = p["rw_w0"][i, d].reshape(16, 64).T
            evp[:, i, 128 + d * 16:128 + d * 16 + 16] = p["rw_a0"][i, d].reshape(16, 64).T
            evp[:, i, 208 + d * 16:208 + d * 16 + 16] = p["rw_gn_g"][i, d].reshape(16, 64).T
            evp[:, i, 240 + d * 16:240 + d * 16 + 16] = p["rw_gn_b"][i, d].reshape(16, 64).T
            for q in range(2):
                evp[:, i, 272 + d * 2 + q] = p["rw_mu_lora"][i, d, q]
        evp[:, i, 160:176] = p["rw_k_k"][i].reshape(16, 64).T
        evp[:, i, 176:192] = p["rw_k_a"][i].reshape(16, 64).T
        evp[:, i, 192:208] = p["rw_r_k"][i].T
    out["evp"] = evp
    evq = np.zeros((128, 2, NPQ), f)
    pk = lambda v: np.asarray(v, f).reshape(8, 128).T
    for i in range(2):
        for d in range(2):
            for q in range(3):
                evq[:, i, (d * 3 + q) * 8:(d * 3 + q) * 8 + 8] = pk(p["rw_mu_rkv"][i, d, q])
            evq[:, i, 48 + d * 8:56 + d * 8] = pk(p["rw_w0"][i, d])
            evq[:, i, 64 + d * 8:72 + d * 8] = pk(p["rw_a0"][i, d])
            evq[:, i, 104 + d * 8:112 + d * 8] = pk(p["rw_gn_g"][i, d])
            evq[:, i, 120 + d * 8:128 + d * 8] = pk(p["rw_gn_b"][i, d])
        evq[:, i, 80:88] = pk(p["rw_k_k"][i])
        evq[:, i, 88:96] = pk(p["rw_k_a"][i])
        evq[:, i, 96:104] = pk(p["rw_r_k"][i].reshape(-1))
    out["evq"] = evq
    out["ps128"] = np.ascontiguousarray(p["pool_scale"].reshape(2, 8, 128).transpose(2, 0, 1)).astype(f)
    qkg = np.zeros((128, 2, 2), f)
    qkg[:, :, 0] = p["q_norm_g"].T
    qkg[:, :, 1] = p["k_norm_g"].T
    out["qkg"] = qkg
    out["modbT"] = np.ascontiguousarray(p["mod_b"].reshape(4, 48, 128).transpose(2, 0, 1)).astype(f)
    for k in ("mod_w", "ln_g", "ln_b", "ev_w_in", "ev_w_out", "pool_w", "rw_w2", "rw_a2", "od_w_in", "od_w_out"):
        out[k] = np.ascontiguousarray(p[k], dtype=f)
    return out


def host_core_inputs(x, c, ctx, c_ctx, core, NB):
    f = np.float32
    b0 = core * NB
    xin = np.concatenate([ctx[b0:b0 + NB], x[b0:b0 + NB]], axis=1).astype(f)
    cc = np.concatenate([c[b0:b0 + NB], c_ctx[None, :]], axis=0)
    cT = np.ascontiguousarray(cc.reshape(NB + 1, KC, 128).transpose(2, 1, 0)).astype(f)
    return {"xin": np.ascontiguousarray(xin), "cT": cT}


_CACHE = {}


def kernel(**inputs):
    x = np.asarray(inputs["x"], np.float32)
    B, T, _ = x.shape
    L = inputs["ctx"].shape[1]
    ncores = 8
    NB = B // ncores
    key = (T, L, NB)
    if key not in _CACHE:
        _CACHE[key] = build(T, L, NB)
    nc = _CACHE[key]
    shared = dict(host_consts(T, L))
    shared.update(host_params({k: np.asarray(v) for k, v in inputs.items()}))
    c = np.asarray(inputs["c"], np.float32)
    ctx = np.asarray(inputs["ctx"], np.float32)
    c_ctx = np.asarray(inputs["c_ctx"], np.float32)
    in_maps = []
    for core in range(ncores):
        m = dict(shared)
        m.update(host_core_inputs(x, c, ctx, c_ctx, core, NB))
        in_maps.append(m)
    res = run_bass_kernel_spmd(nc, in_maps, core_ids=list(range(ncores)))
    return np.concatenate([r["yout"] for r in res.results], axis=0).astype(np.float32)
```

```python
import contextlib
import numpy as np
import concourse.bass as bass
import concourse.mybir as mybir

F32 = mybir.dt.float32
BF16 = mybir.dt.bfloat16
F32R = mybir.dt.float32r
AF = mybir.ActivationFunctionType
ALU = mybir.AluOpType
AX = mybir.AxisListType

SEG = 30000
NDSEM = 12
ENGS = ("pe", "act", "dve", "pool", "sp")


class _Op:
    __slots__ = ("eng", "fn", "waits", "inc", "semval", "snap", "dma", "dwaits")

    def __init__(self, eng, fn, dma):
        self.eng = eng
        self.fn = fn
        self.waits = []
        self.dwaits = []
        self.inc = False
        self.semval = None
        self.snap = None
        self.dma = dma


class Sched:
    def __init__(self, nc):
        self.nc = nc
        self.ops = {e: [] for e in ENGS}
        self.ndma = {e: 0 for e in ENGS}
        self.dma_ops = {e: [] for e in ENGS}
        self.writers = {}
        self.readers = {}
        self.known = {e: {} for e in ENGS}
        self.kdma = {e: set() for e in ENGS}

    def _need(self, op, ev):
        e = op.eng
        if ev[0] == "dma":
            _, q, i = ev
            if (q, i) in self.kdma[e]:
                return
            op.dwaits.append((q, i))
            self.kdma[e].add((q, i))
            src = self.dma_ops[q][i]
        else:
            src_e, n = ev
            if src_e == e and e == "pe":
                return
            if self.known[e].get(src_e, 0) >= n:
                return
            op.waits.append((src_e, n))
            src = self.ops[src_e][n - 1]
            src.inc = True
            self.known[e][src_e] = n
        if src.snap is not None:
            k = self.known[e]
            for se, n2 in src.snap.items():
                if k.get(se, 0) < n2:
                    k[se] = n2

    def _record(self, eng, fn, reads, writes, dma=False):
        op = _Op(eng, fn, None)
        if dma:
            i = self.ndma[eng]
            op.dma = i
            self.ndma[eng] += 1
            self.dma_ops[eng].append(op)
            if i >= NDSEM:
                self._need(op, ("dma", eng, i - NDSEM))
        for k in reads:
            for ev in self.writers.get(k, ()):
                self._need(op, ev)
        for k in writes:
            for ev in self.writers.get(k, ()):
                self._need(op, ev)
            for ev in self.readers.get(k, ()):
                self._need(op, ev)
        self.ops[eng].append(op)
        n = len(self.ops[eng])
        ev = ("dma", eng, op.dma) if dma else (eng, n)
        for k in writes:
            self.writers[k] = [ev]
            self.readers[k] = []
        for k in reads:
            if k in writes:
                continue
            lst = self.readers.setdefault(k, [])
            if ev[0] != "dma":
                lst[:] = [x for x in lst if x[0] != ev[0]]
            lst.append(ev)
        if not dma:
            self.known[eng][eng] = n if eng == "pe" else self.known[eng].get(eng, 0)
        op.snap = dict(self.known[eng])
        return op

    @staticmethod
    def _key(ap):
        return ap.tensor.name

    def _keys(self, aps):
        out = []
        for a in aps:
            if a is None or isinstance(a, (int, float)):
                continue
            if isinstance(a, str) or isinstance(a, tuple):
                out.append(a)
            else:
                out.append(self._key(a))
        return out

    def op(self, eng, method, reads, writes, *args, **kwargs):
        rk = self._keys(reads)
        wk = self._keys(writes)

        def fn(e, method=method, args=args, kwargs=kwargs):
            return getattr(e, method)(*args, **kwargs)

        return self._record(eng, fn, rk, wk)

    def matmul(self, out, lhsT, rhs, start=True, stop=True, **kw):
        return self.op("pe", "matmul", [lhsT, rhs], [out], out, lhsT=lhsT, rhs=rhs, start=start, stop=stop, **kw)

    def transpose(self, out, in_, ident):
        return self.op("pe", "transpose", [in_, ident], [out], out, in_, ident)

    def act(self, out, in_, func, bias=None, scale=None, accum_out=None, eng="act"):
        kw = {}
        rd = [in_]
        if bias is not None:
            kw["bias"] = bias
            rd.append(bias)
        if scale is not None:
            kw["scale"] = scale
            rd.append(scale)
        wr = [out]
        if accum_out is not None:
            kw["accum_out"] = accum_out
            wr.append(accum_out)
        return self.op(eng, "activation", rd, wr, out, in_, func, **kw)

    def tt(self, eng, out, in0, in1, op):
        return self.op(eng, "tensor_tensor", [in0, in1], [out], out, in0, in1, op)

    def ts(self, eng, out, in0, s1, op0, s2=None, op1=None, accum_out=None):
        kw = {}
        if op1 is not None:
            kw["op1"] = op1
        wr = [out]
        if accum_out is not None:
            kw["accum_out"] = accum_out
            wr.append(accum_out)
        return self.op(eng, "tensor_scalar", [in0, s1, s2], wr, out, in0, s1, s2, op0, **kw)

    def stt(self, out, in0, scalar, in1, op0, op1, eng="dve"):
        return self.op(eng, "scalar_tensor_tensor", [in0, scalar, in1], [out], out, in0, scalar, in1, op0, op1)

    def copy(self, eng, out, in_):
        if eng == "act":
            return self.op("act", "copy", [in_], [out], out, in_)
        return self.op(eng, "tensor_copy", [in_], [out], out, in_)

    def memset(self, eng, ap, val):
        return self.op(eng, "memset", [], [ap], ap, val)

    def dma(self, out, in_, q="sp", rkeys=None, wkeys=None, **kw):
        rk = self._keys(rkeys if rkeys is not None else [in_])
        wk = self._keys(wkeys if wkeys is not None else [out])

        def fn(e, out=out, in_=in_, kw=kw):
            return e.dma_start(out=out, in_=in_, **kw)

        return self._record(q, fn, rk, wk, dma=True)

    def begin(self, stack, nseg=8):
        nc = self.nc
        self.csems = {e: [stack.enter_context(nc.semaphore(f"c_{e}_{j}")) for j in range(nseg)] for e in ENGS}
        self.dsems = {e: [stack.enter_context(nc.semaphore(f"d_{e}_{j}")) for j in range(NDSEM)] for e in ("sp", "act", "pool")}
        self.tinc = {e: 0 for e in ENGS}
        self.tdma = {e: 0 for e in ENGS}

    def flush(self):
        nc = self.nc
        csems, dsems = self.csems, self.dsems
        base_inc = dict(self.tinc)
        base_dma = dict(self.tdma)
        for e in ENGS:
            c = base_inc[e]
            for op in self.ops[e]:
                if op.inc and op.dma is None:
                    c += 1
                    op.semval = c
            self.tinc[e] = c
            assert c < SEG * len(csems[e]), "out of compute semaphore segments"
        ops = self.ops
        ndma = self.ndma

        def run(engname, e):
            for op in ops[engname]:
                for (se, n) in op.waits:
                    v = ops[se][n - 1].semval - 1
                    e.wait_ge(csems[se][v // SEG], v % SEG + 1)
                for (q, i) in op.dwaits:
                    g = base_dma[q] + i
                    e.wait_ge(dsems[q][g % NDSEM], 16 * (g // NDSEM + 1))
                ins = op.fn(e)
                if op.dma is not None:
                    g = base_dma[engname] + op.dma
                    ins.then_inc(dsems[engname][g % NDSEM], 16)
                elif op.inc:
                    v = op.semval - 1
                    ins.then_inc(csems[engname][v // SEG], 1)
            n = ndma[engname]
            for i in range(max(0, n - NDSEM), n):
                g = base_dma[engname] + i
                e.wait_ge(dsems[engname][g % NDSEM], 16 * (g // NDSEM + 1))

        with nc.Block() as block:
            @block.tensor
            def _(e):
                run("pe", e)

            @block.scalar
            def _(e):
                run("act", e)

            @block.vector
            def _(e):
                run("dve", e)

            @block.gpsimd
            def _(e):
                run("pool", e)

            @block.sync
            def _(e):
                run("sp", e)
        for e in ENGS:
            self.tdma[e] += self.ndma[e]
        self.ops = {e: [] for e in ENGS}
        self.ndma = {e: 0 for e in ENGS}
        self.dma_ops = {e: [] for e in ENGS}
        self.writers = {}
        self.readers = {}
        self.known = {e: {} for e in ENGS}
        self.kdma = {e: set() for e in ENGS}

from concourse.bass_utils import run_bass_kernel_spmd

D = 2048
KC = 16
EVIN = 6400
ODIN = 5120
ALPHA = 8 ** 0.25
CDEC = float(np.exp(-0.5))
NPAR = 276
NPQ = 136
POOLW = (2, 4, 8, 16)


def _blocks(L, T, bw=512):
    out = []
    for (s0, ln) in ((0, L), (L, T)):
        t = s0
        while t < s0 + ln:
            w = min(bw, s0 + ln - t)
            out.append((s0, ln, t, w))
            t += w
    return out


def build(T, L, NB, depth=4, upto=99):
    N = L + T
    NC3 = NB + 1
    NCH = N // 64
    LCH = L // 64
    nc = bass.Bass("TRN2", target_bir_lowering=False)
    din = lambda name, shape, dt=F32: nc.dram_tensor(name, list(shape), dt, kind="ExternalInput").ap()
    dsc = lambda name, shape, dt=F32: nc.dram_tensor(name, list(shape), dt).ap()
    xin = din("xin", [NB, N, D])
    cT = din("cT", [128, KC, NC3])
    mod_w = din("mod_w", [4, D, 3 * D])
    modbT = din("modbT", [128, 4, 48])
    ln_g = din("ln_g", [4, D])
    ln_b = din("ln_b", [4, D])
    ev_w_in = din("ev_w_in", [2, D, EVIN])
    ev_w_out = din("ev_w_out", [2, D, D])
    pool_w = din("pool_w", [2, 4, 256, 256])
    evp = din("evp", [64, 2, NPAR])
    evq = din("evq", [128, 2, NPQ])
    ps128 = din("ps128", [128, 2, 8])
    rw_w2 = din("rw_w2", [2, 2, 64, 1024])
    rw_a2 = din("rw_a2", [2, 2, 64, 1024])
    od_w_in = din("od_w_in", [2, D, ODIN])
    od_w_out = din("od_w_out", [2, D, D])
    qkg = din("qkg", [128, 2, 2])
    c_ident = din("c_ident", [128, 128])
    c_rt = din("c_rt", [128, 128])
    c_mk8 = din("c_mk8", [64, 8, 4, 64])
    c_mn8 = din("c_mn8", [64, 8, 64])
    c_cos = din("c_cos", [128, T])
    c_sin = din("c_sin", [128, T])
    c_rcnt = din("c_rcnt", [4, N])
    yout = nc.dram_tensor("yout", [NB, T, D], F32, kind="ExternalOutput").ap()
    XS = dsc("XS", [NB, N, D])
    PT = dsc("PT", [EVIN, N])
    QT = dsc("QT", [2048 + 512, N], BF16)
    VT = dsc("VT", [N, 512], BF16)
    MT = dsc("MT", [D, N], BF16)
    OPS = dsc("OPS", [2, 16, 5, 64, N])
    GAM = dsc("GAM", [2, 16, 64, NCH])
    BON = dsc("BON", [2, 16, 64, N])
    YT = dsc("YT", [2, 16, 64, N])

    S = Sched(nc)
    top = contextlib.ExitStack()
    S.begin(top)
    uid = [0]

    def sb(st, shape, dt=F32, name="t"):
        uid[0] += 1
        return st.enter_context(nc.sbuf_tensor(f"{name}{uid[0]}", list(shape), dt))

    def ps(st, shape, dt=F32, name="p"):
        uid[0] += 1
        return st.enter_context(nc.psum_tensor(f"{name}{uid[0]}", list(shape), dt))

    ukey = [0]

    def uk(tag):
        ukey[0] += 1
        return (tag, ukey[0])

    rr = [0]

    def EW():
        rr[0] += 1
        return "pool" if rr[0] % 3 == 0 else "dve"

    ident = sb(top, [128, 128], name="ident")
    onesf = sb(top, [128, 128], name="onesf")
    modT = sb(top, [128, depth, 48, NC3], name="modT")
    cst = sb(top, [128, 4], name="cst")
    blk1 = sb(top, [128, 128], name="blk1")

    with contextlib.ExitStack() as st:
        S.dma(ident[:], c_ident[:, :])
        S.memset("dve", onesf[:], 1.0)
        S.memset("dve", cst[:, 0:1], 1e-12)
        S.memset("pool", blk1[:], 0.0)
        S.memset("pool", blk1[0:64, 0:64], 1.0)
        S.memset("pool", blk1[64:128, 64:128], 1.0)
        S.memset("dve", cst[:, 1:2], 64e-5)
        S.memset("dve", cst[:, 2:3], 1e-6)
        ct = sb(st, [128, KC, NC3])
        sc = sb(st, [128, KC, NC3])
        mb = sb(st, [128, 4, 48])
        S.dma(ct[:], cT[:, :, :])
        S.dma(mb[:], modbT[:, :, :])
        S.act(sc[:], ct[:], AF.Silu)
        wsts = [sb(st, [128, KC, 512]) for _ in range(2)]
        pm = [ps(st, [128, 8]) for _ in range(2)]
        for l in range(depth):
            for cb in range(12):
                w = wsts[cb % 2]
                S.dma(w[:], mod_w[l, :, cb * 512:(cb + 1) * 512].rearrange("(k p) c -> p k c", p=128))
                for m in range(4):
                    j = cb * 4 + m
                    p = pm[j % 2]
                    for k in range(KC):
                        S.matmul(p[:, 0:NC3], lhsT=w[:, k, m * 128:(m + 1) * 128], rhs=sc[:, k, :],
                                 start=(k == 0), stop=(k == KC - 1))
                    S.act(modT[:, l, j, :], p[:, 0:NC3], AF.Identity, bias=mb[:, l, j:j + 1])
            S.ts("dve", modT[:, l, 16:32, :], modT[:, l, 16:32, :], 1.0, ALU.add)
        S.flush()

    def build_hT(st, b, l):
        src = xin if l == 0 else XS
        hT = [sb(st, [128, N], BF16, name="hT") for _ in range(KC)]
        xt = [sb(st, [128, D]) for _ in range(2)]
        pT = [ps(st, [128, 512]) for _ in range(2)]
        for tt in range(N // 128):
            col = NB if tt * 128 < L else b
            x = xt[tt % 2]
            S.dma(x[:], src[b, tt * 128:(tt + 1) * 128, :], rkeys=[("X", tt)])
            for k4 in range(4):
                p = pT[(tt * 4 + k4) % 2]
                for j in range(4):
                    k = k4 * 4 + j
                    S.transpose(p[:, j * 128:(j + 1) * 128], x[:, k * 128:(k + 1) * 128], ident[:])
                for j in range(4):
                    k = k4 * 4 + j
                    o = hT[k][:, tt * 128:(tt + 1) * 128]
                    if (tt * 4 + k4) % 2 == 0:
                        S.act(o, p[:, j * 128:(j + 1) * 128], AF.Identity,
                              scale=modT[:, l, 16 + k, col:col + 1], bias=modT[:, l, k, col:col + 1])
                    else:
                        S.ts("dve", o, p[:, j * 128:(j + 1) * 128], modT[:, l, 16 + k, col:col + 1], ALU.mult,
                             modT[:, l, k, col:col + 1], ALU.add)
        return hT

    tblocks = []
    t = 0
    while t < N:
        w = min(512, N - t)
        tblocks.append((t, w))
        t += w

    def phase_inproj_even(b, l):
        i = l // 2
        with contextlib.ExitStack() as st:
            hT = build_hT(st, b, l)
            wst = sb(st, [128, KC, 256])
            wbf = [sb(st, [128, KC, 256], BF16) for _ in range(2)]
            pp = [ps(st, [128, 512]) for _ in range(4)]
            stg = [sb(st, [128, 512]) for _ in range(4)]
            cnt = 0

            def loadw(cb):
                S.dma(wst[:], ev_w_in[i, :, cb * 256:(cb + 1) * 256].rearrange("(k p) c -> p k c", p=128))
                S.copy("pool", wbf[cb % 2][:], wst[:])

            loadw(0)
            for cb in range(EVIN // 256):
                if cb + 1 < EVIN // 256:
                    loadw(cb + 1)
                wb = wbf[cb % 2]
                for m in range(2):
                    for (t0, w) in tblocks:
                        p = pp[cnt % 4]
                        sg = stg[cnt % 4]
                        for k in range(KC):
                            S.matmul(p[:, 0:w], lhsT=wb[:, k, m * 128:(m + 1) * 128], rhs=hT[k][:, t0:t0 + w],
                                     start=(k == 0), stop=(k == KC - 1))
                        S.copy("act" if cnt % 2 == 0 else "dve", sg[:, 0:w], p[:, 0:w])
                        r0 = cb * 256 + m * 128
                        S.dma(PT[r0:r0 + 128, t0:t0 + w], sg[:, 0:w], wkeys=[uk("PT")])
                        cnt += 1
            S.flush()

    def phase_pool(b, l):
        i = l // 2
        last = l == depth - 1
        with contextlib.ExitStack() as st:
            psc = sb(st, [128, 8])
            S.dma(psc[:], ps128[:, i, :])
            pw = sb(st, [128, 2, 256])
            pwb = sb(st, [128, 2, 256], BF16)
            mk2 = lambda shape, dt=F32: [[sb(st, shape, dt) for _ in range(2)] for _ in range(2)]
            ex2, wk2 = mk2([128, 528]), mk2([128, 528])
            rc2 = [sb(st, [128, 512]) for _ in range(2)]
            pb2 = mk2([128, 512], BF16)
            gt2, sgt2 = mk2([128, 512]), mk2([128, 512])
            ob2 = mk2([128, 512], BF16)
            pq2 = [[ps(st, [128, 512]) for _ in range(2)] for _ in range(2)]
            itn = [0]
            for g in range(4):
                nadd = g + 1
                h = POOLW[g] // 2
                S.dma(pw[:], pool_w[i, g].rearrange("(c p) d -> p c d", p=128))
                S.copy("pool", pwb[:], pw[:])
                for (s0, ln, t0, w) in _blocks(L, T):
                    if last and s0 == 0:
                        continue
                    ip = itn[0] % 2
                    itn[0] += 1
                    ex, wk, rc, pb, gt, sgt, ob, pq = ex2[ip], wk2[ip], rc2[ip], pb2[ip], gt2[ip], sgt2[ip], ob2[ip], pq2[ip]
                    S.dma(rc[:, 0:w], c_rcnt[g:g + 1, t0:t0 + w].partition_broadcast(128))
                    for cc in range(2):
                        j = g * 2 + cc
                        e = ex[cc]
                        lo = max(s0, t0 - 8)
                        hi = min(s0 + ln, t0 + w + 8)
                        S.memset("pool", e[:, 0:w + 16], 0.0)
                        S.dma(e[:, 8 - (t0 - lo):8 - (t0 - lo) + (hi - lo)], PT[j * 128:(j + 1) * 128, lo:hi])
                        cur = e
                        width = w + 16
                        for a in range(nadd):
                            sh = 1 << a
                            nw = width - sh
                            dst = wk[a % 2]
                            S.tt("dve", dst[:, 0:nw], cur[:, 0:nw], cur[:, sh:sh + nw], ALU.add)
                            cur = dst
                            width = nw
                        tmp = wk[nadd % 2]
                        S.tt("dve", tmp[:, 0:w], cur[:, 8 - h:8 - h + w], rc[:, 0:w], ALU.mult)
                        S.tt("pool", pb[cc][:, 0:w], tmp[:, 0:w], e[:, 8:8 + w], ALU.subtract)
                    for dj in range(2):
                        jj = g * 2 + dj
                        p = pq[dj]
                        for cc in range(2):
                            S.matmul(p[:, 0:w], lhsT=pwb[:, cc, dj * 128:(dj + 1) * 128], rhs=pb[cc][:, 0:w],
                                     start=(cc == 0), stop=(cc == 1))
                        S.dma(gt[dj][:, 0:w], PT[4352 + jj * 128:4352 + (jj + 1) * 128, t0:t0 + w])
                        S.act(sgt[dj][:, 0:w], gt[dj][:, 0:w], AF.Silu)
                        S.stt(ob[dj][:, 0:w], p[:, 0:w], psc[:, jj:jj + 1], sgt[dj][:, 0:w], ALU.mult, ALU.mult)
                        S.dma(MT[jj * 128:(jj + 1) * 128, t0:t0 + w], ob[dj][:, 0:w], wkeys=[uk("MT")])
            S.flush()

    def phase_rwkv_prep(b, l):
        i = l // 2
        with contextlib.ExitStack() as st:
            par = sb(st, [64, NPAR])
            S.dma(par[:], evp[:, i, :])
            parq = sb(st, [128, NPQ])
            S.dma(parq[:], evq[:, i, :])
            w2t = [sb(st, [64, 1024]) for _ in range(2)]
            a2t = [sb(st, [64, 1024]) for _ in range(2)]
            for d in range(2):
                S.dma(w2t[d][:], rw_w2[i, d, :, :])
                S.dma(a2t[d][:], rw_a2[i, d, :, :])
            ones1 = sb(st, [128, 512])
            S.memset("dve", ones1[:], 1.0)
            T_ = lambda n=514: sb(st, [128, n])
            T64 = lambda n=514: sb(st, [64, n])
            wdx = [T64() for _ in range(2)]
            adx = [T64() for _ in range(2)]
            tw = [T64(512) for _ in range(2)]
            sa = [T64(512) for _ in range(2)]
            tl = [T64(512) for _ in range(2)]
            x3 = [[T_() for _ in range(3)] for _ in range(2)]
            tmp = [T_(512) for _ in range(4)]
            rkv = [[T_(512) for _ in range(3)] for _ in range(2)]
            sig = [T_(512) for _ in range(2)]
            asg = [T_(512) for _ in range(2)]
            kk = [T_(512) for _ in range(2)]
            sq = [T_(512) for _ in range(2)]
            rn = [T_(512) for _ in range(2)]
            kkn = [T_(512) for _ in range(2)]
            kmod = [T_(512) for _ in range(2)]
            bvec = [T_(512) for _ in range(2)]
            rk = [T_(512) for _ in range(2)]
            bon = [T_(512) for _ in range(2)]
            pe_ = [T_(580) for _ in range(2)]
            gs = [T_(512) for _ in range(2)]
            gm = [T_(512) for _ in range(2)]
            eG = [T_(512) for _ in range(2)]
            eGi = [T_(512) for _ in range(2)]
            eGm = [T_(512) for _ in range(2)]
            oq = [[T_(512) for _ in range(4)] for _ in range(2)]
            pw_ = [ps(st, [128, 512]) for _ in range(2)]
            pa_ = [ps(st, [128, 512]) for _ in range(2)]
            pss = [ps(st, [128, 512]) for _ in range(2)]
            pbn = [ps(st, [128, 512]) for _ in range(2)]
            for d in range(2):
                S.memset("dve", pe_[d][:, 0:1], 0.0)

            def load_halo(dst, row0, s0, ln, t0, w, nr=64):
                lo = max(s0, t0 - 1)
                hi = min(s0 + ln, t0 + w + 1)
                if lo > t0 - 1:
                    S.memset("pool", dst[:, 0:1], 0.0)
                if hi < t0 + w + 1:
                    S.memset("pool", dst[:, w + 1:w + 2], 0.0)
                o = 1 - (t0 - lo)
                S.dma(dst[:, o:o + hi - lo], PT[row0:row0 + nr, lo:hi])

            def shiftmix(out, X, d, w, mucol, scratch):
                nbv = X[:, 0:w] if d == 0 else X[:, 2:w + 2]
                cur = X[:, 1:w + 1]
                S.tt(EW(), scratch[:, 0:w], nbv, cur, ALU.subtract)
                S.stt(out, scratch[:, 0:w], mucol, cur, ALU.mult, ALU.add)

            it = 0
            ewc = [0]

            def EW():
                ewc[0] += 1
                return "dve" if ewc[0] % 10 in (0, 3, 7) else "pool"

            for (s0, ln, t0, w) in _blocks(L, T):
                nchb = w // 64
                c0 = t0 // 64
                for d in range(2):
                    load_halo(wdx[d], 4096 + 64 * d, s0, ln, t0, w)
                    load_halo(adx[d], 4224 + 64 * d, s0, ln, t0, w)
                    shiftmix(tl[0][:, 0:w], wdx[d], d, w, par[:, 272 + d * 2:273 + d * 2], tl[1])
                    S.act(tw[d][:, 0:w], tl[0][:, 0:w], AF.Tanh)
                    shiftmix(sa[d][:, 0:w], adx[d], d, w, par[:, 273 + d * 2:274 + d * 2], tl[1])
                for h in range(8):
                    xs_ = x3[h % 2]
                    for q in range(3):
                        load_halo(xs_[q], 1024 * (q + 1) + 128 * h, s0, ln, t0, w, nr=128)
                    for d in range(2):
                        z = it % 2
                        it += 1
                        r, k, v = (rkv[z][q][:, 0:w] for q in range(3))
                        for q, o in enumerate((r, k, v)):
                            c = (d * 3 + q) * 8 + h
                            shiftmix(o, xs_[q], d, w, parq[:, c:c + 1], tmp[2 + (q % 2)])
                        S.matmul(pw_[z][:, 0:w], lhsT=w2t[d][:, 128 * h:128 * h + 128], rhs=tw[d][:, 0:w])
                        S.act(sig[z][:, 0:w], pw_[z][:, 0:w], AF.Sigmoid, bias=parq[:, 48 + d * 8 + h:49 + d * 8 + h])
                        S.matmul(pa_[z][:, 0:w], lhsT=a2t[d][:, 128 * h:128 * h + 128], rhs=sa[d][:, 0:w])
                        S.act(asg[z][:, 0:w], pa_[z][:, 0:w], AF.Sigmoid, bias=parq[:, 64 + d * 8 + h:65 + d * 8 + h])
                        S.act(kk[z][:, 0:w], k, AF.Identity, scale=parq[:, 80 + h:81 + h])
                        S.act(sq[z][:, 0:w], k, AF.Square, scale=parq[:, 80 + h:81 + h])
                        S.matmul(pss[z][:, 0:w], lhsT=blk1[:], rhs=sq[z][:, 0:w])
                        S.act(rn[z][:, 0:w], pss[z][:, 0:w], AF.Sqrt, bias=cst[:, 0:1])
                        S.op("dve", "reciprocal", [rn[z][:]], [rn[z][:]], rn[z][:, 0:w], rn[z][:, 0:w])
                        S.tt(EW(), kkn[z][:, 0:w], kk[z][:, 0:w], rn[z][:, 0:w], ALU.mult)
                        S.ts(EW(), tmp[0][:, 0:w], asg[z][:, 0:w], -1.0, ALU.add, parq[:, 88 + h:89 + h], ALU.mult)
                        S.stt(kmod[z][:, 0:w], tmp[0][:, 0:w], 1.0, k, ALU.add, ALU.mult)
                        S.tt(EW(), bvec[z][:, 0:w], kkn[z][:, 0:w], asg[z][:, 0:w], ALU.mult)
                        S.stt(rk[z][:, 0:w], r, parq[:, 96 + h:97 + h], kmod[z][:, 0:w], ALU.mult, ALU.mult)
                        S.matmul(pbn[z][:, 0:w], lhsT=blk1[:], rhs=rk[z][:, 0:w])
                        S.tt("dve", bon[z][:, 0:w], pbn[z][:, 0:w], v, ALU.mult)
                        S.dma(BON[d].rearrange("h p n -> (h p) n")[128 * h:128 * h + 128, t0:t0 + w], bon[z][:, 0:w], wkeys=[uk("BON")])
                        P = pe_[z]
                        S.op("dve", "tensor_tensor_scan", [ones1[:], sig[z][:]], [P[:]], P[:, 1:w + 1], ones1[:, 0:w],
                             sig[z][:, 0:w], 0.0, ALU.mult, ALU.add)
                        g3 = gs[z][:, 0:w].rearrange("p (c j) -> p c j", j=64)
                        if d == 0:
                            a0_ = P[:, 1:w + 1].rearrange("p (c j) -> p c j", j=64)
                            b0_ = P[:, 0:w].rearrange("p (c j) -> p c j", j=64)[:, :, 0:1].broadcast_to([128, nchb, 64])
                            S.tt("dve", g3, a0_, b0_, ALU.subtract)
                        else:
                            a0_ = P[:, 64:w + 64].rearrange("p (c j) -> p c j", j=64)[:, :, 0:1].broadcast_to([128, nchb, 64])
                            b0_ = P[:, 0:w].rearrange("p (c j) -> p c j", j=64)
                            S.tt("dve", g3, a0_, b0_, ALU.subtract)
                        S.tt(EW(), gm[z][:, 0:w], gs[z][:, 0:w], sig[z][:, 0:w], ALU.subtract)
                        S.act(eG[z][:, 0:w], gs[z][:, 0:w], AF.Exp, scale=-CDEC)
                        S.act(eGi[z][:, 0:w], gs[z][:, 0:w], AF.Exp, scale=CDEC)
                        S.act(eGm[z][:, 0:w], gm[z][:, 0:w], AF.Exp, scale=-CDEC)
                        o4 = oq[z]
                        S.stt(o4[0][:, 0:w], kkn[z][:, 0:w], -1.0, eGm[z][:, 0:w], ALU.mult, ALU.mult)
                        S.tt(EW(), o4[1][:, 0:w], r, eG[z][:, 0:w], ALU.mult)
                        S.tt(EW(), o4[2][:, 0:w], bvec[z][:, 0:w], eGi[z][:, 0:w], ALU.mult)
                        S.tt(EW(), o4[3][:, 0:w], kmod[z][:, 0:w], eGi[z][:, 0:w], ALU.mult)
                        for q in range(4):
                            for hh in range(2):
                                S.dma(OPS[d, 2 * h + hh, q, :, t0:t0 + w], o4[q][64 * hh:64 * hh + 64, 0:w], wkeys=[uk("OPS")])
                        for hh in range(2):
                            S.dma(OPS[d, 2 * h + hh, 4, :, t0:t0 + w], rkv[z][2][64 * hh:64 * hh + 64, 0:w], wkeys=[uk("OPS")])
                        e3 = eG[z][:, 0:w].rearrange("p (c j) -> p c j", j=64)
                        gsrc = e3[:, :, 63] if d == 0 else e3[:, :, 0]
                        S.copy("pool", tmp[1][:, 0:nchb], gsrc)
                        S.dma(GAM[d].rearrange("h p c -> (h p) c")[128 * h:128 * h + 128, c0:c0 + nchb], tmp[1][:, 0:nchb], wkeys=[uk("GAM")])
            S.flush()

    def phase_rwkv_scan(b, l):
        order = [list(range(NCH)), list(range(LCH - 1, -1, -1)) + list(range(NCH - 1, LCH - 1, -1))]
        with contextlib.ExitStack() as st:
            mk8 = sb(st, [64, 8, 4, 64])
            mn8 = sb(st, [64, 8, 64])
            S.dma(mk8[:], c_mk8[:, :, :, :])
            S.dma(mn8[:], c_mn8[:, :, :])
            OPb = [[sb(st, [64, 4, 5, 64]) for _ in range(2)] for _ in range(3)]
            OPr = [[sb(st, [64, 4, 4, 64], F32R) for _ in range(2)] for _ in range(3)]
            gam = [sb(st, [64, 4, NCH]) for _ in range(2)]
            TM2 = [sb(st, [64, 8, 3, 64], F32R) for _ in range(2)]
            AM2 = [sb(st, [64, 8, 4, 64], F32R) for _ in range(2)]
            Xs2 = [[None] + [sb(st, [64, 8, 64], F32R) for _ in range(5)] for _ in range(2)]
            Ys2 = [[sb(st, [64, 8, 64], F32R) for _ in range(5)] for _ in range(2)]
            U = sb(st, [64, 8, 64], F32R)
            Hr = sb(st, [64, 8, 64], F32R)
            H1 = sb(st, [64, 8, 64])
            ys = [sb(st, [64, 8, 64]) for _ in range(2)]
            big = ps(st, [64, 2048])
            PX = ps(st, [64, 512])
            PY = ps(st, [64, 512])
            pa = ps(st, [64, 512])
            pb_ = ps(st, [64, 512])
            A1 = big[:].rearrange("p (v c) -> p v c", c=256)
            TMp = big[:, 0:1536].rearrange("p (v q c) -> p v q c", q=3, c=64)
            v3 = lambda t_: t_[:].rearrange("p (v c) -> p v c", c=64)
            mkf = mk8[:].rearrange("p v q c -> p (v q c)")
            for g4 in range(4):
                h0 = g4 * 4
                for d in range(2):
                    S.dma(gam[d][:], GAM[d, h0:h0 + 4].rearrange("h p c -> p h c"), rkeys=[])
                S.memset("dve", H1[:], 0.0)
                S.copy("pool", Hr[:], H1[:])

                def load(s_):
                    o3 = s_ % 3
                    for d in range(2):
                        c = order[d][s_]
                        S.dma(OPb[o3][d][:],
                              OPS[d, h0:h0 + 4, :, :, c * 64:(c + 1) * 64].rearrange("h q p c -> p h q c"), rkeys=[])
                        S.copy("pool", OPr[o3][d][:], OPb[o3][d][:, :, 0:4, :])

                def Xl(s_):
                    p2 = s_ % 2
                    return [AM2[p2][:, :, 0, :]] + [x_[:] for x_ in Xs2[p2][1:]]

                def a_stages(s_):
                    p2, o3 = s_ % 2, s_ % 3
                    TM, AM, Xs, Ys = TM2[p2], AM2[p2], Xs2[p2], Ys2[p2]
                    O32 = lambda vh, q: OPb[o3][vh // 4][:, vh % 4, q, :]
                    O = lambda vh, q: OPr[o3][vh // 4][:, vh % 4, q, :]
                    X = Xl(s_)

                    def a0():
                        for vh in range(8):
                            for qi in range(3):
                                S.transpose(TMp[:, vh, qi, :], O32(vh, 2 + qi), ident[0:64, 0:64])
                        TMf = TM[:].rearrange("p v q c -> p (v q c)")
                        for q3 in range(3):
                            S.copy("act", TMf[:, q3 * 512:(q3 + 1) * 512], big[:, q3 * 512:(q3 + 1) * 512])

                    def a1():
                        for vh in range(8):
                            ar = OPr[o3][vh // 4][:, vh % 4, 0:2, :].rearrange("p a c -> p (a c)")
                            S.matmul(A1[:, vh, 0:128], lhsT=O(vh, 2), rhs=ar)
                            S.matmul(A1[:, vh, 128:256], lhsT=O(vh, 3), rhs=ar)
                            S.matmul(v3(PX)[:, vh, :], lhsT=O(vh, 0), rhs=O(vh, 2))
                        AMf = AM[:].rearrange("p v q c -> p (v q c)")
                        for q4 in range(4):
                            S.tt("dve", AMf[:, q4 * 512:(q4 + 1) * 512], big[:, q4 * 512:(q4 + 1) * 512],
                                 mkf[:, q4 * 512:(q4 + 1) * 512], ALU.mult)
                        S.tt("dve", Ys[0][:], v3(PX), mn8[:], ALU.mult)

                    def asq(i_):
                        def f():
                            for vh in range(8):
                                S.matmul(v3(PX)[:, vh, :], lhsT=Ys[i_ - 1][:, vh, :], rhs=X[i_ - 1][:, vh, :])
                            if i_ < 5:
                                for vh in range(8):
                                    S.matmul(v3(PY)[:, vh, :], lhsT=X[i_ - 1][:, vh, :], rhs=Ys[i_ - 1][:, vh, :])
                            S.copy("act", Xs[i_][:], v3(PX))
                            if i_ < 5:
                                S.copy("dve", Ys[i_][:], v3(PY))
                        return f

                    return [a0, a1] + [asq(i_) for i_ in range(1, 6)]

                def b_stages(s_):
                    p2, o3 = s_ % 2, s_ % 3
                    TM, AM = TM2[p2], AM2[p2]
                    O = lambda vh, q: OPr[o3][vh // 4][:, vh % 4, q, :]
                    X = Xl(s_)
                    cs = [order[0][s_], order[1][s_]]

                    def b0():
                        for vh in range(8):
                            S.matmul(v3(pa)[:, vh, :], lhsT=O(vh, 0), rhs=Hr[:, vh, :], start=True, stop=False)
                            S.matmul(v3(pa)[:, vh, :], lhsT=AM[:, vh, 2, :], rhs=TM[:, vh, 2, :], start=False, stop=True)
                        S.copy("dve", U[:], v3(pa))

                    def bap(i_):
                        def f():
                            for vh in range(8):
                                S.matmul(v3(pb_)[:, vh, :], lhsT=X[i_][:, vh, :], rhs=U[:, vh, :])
                            S.tt("dve", U[:], U[:].bitcast(F32), v3(pb_), ALU.add)
                        return f

                    def b7():
                        for vh in range(8):
                            o = v3(pa)[:, vh, :]
                            S.matmul(o, lhsT=Hr[:, vh, :], rhs=O(vh, 1), start=True, stop=False)
                            S.matmul(o, lhsT=U[:, vh, :], rhs=AM[:, vh, 1, :], start=False, stop=False)
                            S.matmul(o, lhsT=TM[:, vh, 2, :], rhs=AM[:, vh, 3, :], start=False, stop=True)
                        y_ = ys[s_ % 2]
                        S.copy("act", y_[:], v3(pa))
                        for d in range(2):
                            c = cs[d]
                            S.dma(YT[d, h0:h0 + 4, :, c * 64:(c + 1) * 64].rearrange("h p c -> p h c"),
                                  y_[:, d * 4:(d + 1) * 4, :], wkeys=[uk("YT")])

                    def b8():
                        for vh in range(8):
                            o = v3(pb_)[:, vh, :]
                            S.matmul(o, lhsT=TM[:, vh, 0, :], rhs=U[:, vh, :], start=True, stop=False)
                            S.matmul(o, lhsT=TM[:, vh, 1, :], rhs=TM[:, vh, 2, :], start=False, stop=True)
                        S.tt("dve", H1[:], Hr[:].bitcast(F32), v3(pb_), ALU.add)
                        for d in range(2):
                            c = cs[d]
                            S.tt("dve", Hr[:, d * 4:(d + 1) * 4, :], H1[:, d * 4:(d + 1) * 4, :],
                                 gam[d][:, :, c:c + 1].broadcast_to([64, 4, 64]), ALU.mult)

                    return [b0] + [bap(i_) for i_ in range(6)] + [b7, b8]

                load(0)
                if NCH > 1:
                    load(1)
                for f in a_stages(0):
                    f()
                for s_ in range(NCH):
                    if s_ + 2 < NCH:
                        load(s_ + 2)
                    Bs = b_stages(s_)
                    As = a_stages(s_ + 1) if s_ + 1 < NCH else []
                    for i_ in range(max(len(Bs), len(As))):
                        if i_ < len(Bs):
                            Bs[i_]()
                        if i_ < len(As):
                            As[i_]()
            S.flush()

    def phase_rwkv_out(b, l):
        i = l // 2
        last = l == depth - 1
        with contextlib.ExitStack() as st:
            par = sb(st, [128, NPQ])
            S.dma(par[:], evq[:, i, :])
            yb2 = [[sb(st, [128, 512]) for _ in range(2)] for _ in range(2)]
            bb2 = [[sb(st, [128, 512]) for _ in range(2)] for _ in range(2)]
            yc2 = [[sb(st, [128, 512]) for _ in range(2)] for _ in range(2)]
            sq2 = [[sb(st, [128, 512]) for _ in range(2)] for _ in range(2)]
            rs2 = [[sb(st, [128, 512]) for _ in range(2)] for _ in range(2)]
            acc = [sb(st, [128, 512]) for _ in range(2)]
            gt = [sb(st, [128, 512]) for _ in range(2)]
            sg = [sb(st, [128, 512]) for _ in range(2)]
            ob = [sb(st, [128, 512], BF16) for _ in range(2)]
            pm2 = [[ps(st, [128, 512]) for _ in range(2)] for _ in range(2)]
            pv2 = [[ps(st, [128, 512]) for _ in range(2)] for _ in range(2)]
            it = 0
            for h in range(8):
                for (s0, ln, t0, w) in _blocks(L, T):
                    if last and s0 == 0:
                        continue
                    zz = it % 2
                    it += 1
                    yb, bb, yc, sq, rs, pm, pv = yb2[zz], bb2[zz], yc2[zz], sq2[zz], rs2[zz], pm2[zz], pv2[zz]
                    R2 = range(2)
                    for d in R2:
                        S.dma(yb[d][:, 0:w], YT[d].rearrange("h p n -> (h p) n")[128 * h:128 * h + 128, t0:t0 + w])
                        S.dma(bb[d][:, 0:w], BON[d].rearrange("h p n -> (h p) n")[128 * h:128 * h + 128, t0:t0 + w])
                    for d in R2:
                        S.matmul(pm[d][:, 0:w], lhsT=blk1[:], rhs=yb[d][:, 0:w])
                    for d in R2:
                        S.stt(yc[d][:, 0:w], pm[d][:, 0:w], -1.0 / 64, yb[d][:, 0:w], ALU.mult, ALU.add)
                    for d in R2:
                        S.tt("pool" if d else "dve", sq[d][:, 0:w], yc[d][:, 0:w], yc[d][:, 0:w], ALU.mult)
                    for d in R2:
                        S.matmul(pv[d][:, 0:w], lhsT=blk1[:], rhs=sq[d][:, 0:w])
                    for d in R2:
                        S.act(rs[d][:, 0:w], pv[d][:, 0:w], AF.Sqrt, bias=cst[:, 1:2], scale=1.0 / 64)
                    for d in R2:
                        S.op("dve", "reciprocal", [rs[d][:]], [rs[d][:]], rs[d][:, 0:w], rs[d][:, 0:w])
                    for d in R2:
                        S.tt("pool" if d else "dve", yc[d][:, 0:w], yc[d][:, 0:w], rs[d][:, 0:w], ALU.mult)
                    for d in R2:
                        S.ts("pool" if d else "dve", yc[d][:, 0:w], yc[d][:, 0:w],
                             par[:, 104 + d * 8 + h:105 + d * 8 + h], ALU.mult,
                             par[:, 120 + d * 8 + h:121 + d * 8 + h], ALU.add)
                    for d in R2:
                        S.tt("pool" if d else "dve", yc[d][:, 0:w], yc[d][:, 0:w], bb[d][:, 0:w], ALU.add)
                    S.tt(EW(), acc[zz][:, 0:w], yc[0][:, 0:w], yc[1][:, 0:w], ALU.add)
                    r0 = 4352 + 1024 + 128 * h
                    S.dma(gt[zz][:, 0:w], PT[r0:r0 + 128, t0:t0 + w])
                    S.act(sg[zz][:, 0:w], gt[zz][:, 0:w], AF.Silu)
                    S.tt(EW(), ob[zz][:, 0:w], acc[zz][:, 0:w], sg[zz][:, 0:w], ALU.mult)
                    S.dma(MT[1024 + 128 * h:1024 + 128 * h + 128, t0:t0 + w], ob[zz][:, 0:w], wkeys=[uk("MT")])
            S.flush()

    def phase_inproj_odd(b, l):
        i = l // 2
        with contextlib.ExitStack() as st:
            hT = build_hT(st, b, l)
            wst = sb(st, [128, KC, 256])
            wbf = [sb(st, [128, KC, 256], BF16) for _ in range(2)]
            pp = [ps(st, [128, 512]) for _ in range(3)]
            pss = ps(st, [128, 512])
            prx = ps(st, [128, 512])
            stg = [sb(st, [128, 512]) for _ in range(2)]
            rt = sb(st, [128, 128])
            gq = sb(st, [128, 2])
            cosT = sb(st, [128, T])
            sinT = sb(st, [128, T])
            S.dma(rt[:], c_rt[:, :])
            S.dma(gq[:], qkg[:, i, :])
            S.dma(cosT[:], c_cos[:, :])
            S.dma(sinT[:], c_sin[:, :])
            xg = [sb(st, [128, 512], F32R) for _ in range(2)]
            sq = [sb(st, [128, 512], F32R) for _ in range(2)]
            rtr = sb(st, [128, 128], F32R)
            onr = sb(st, [128, 128], F32R)
            S.copy("dve", rtr[:], rt[:])
            S.copy("dve", onr[:], onesf[:])
            rs = [sb(st, [128, 512]) for _ in range(2)]
            t1 = [sb(st, [128, 512]) for _ in range(2)]
            t2 = [sb(st, [128, 512]) for _ in range(2)]
            ob = [sb(st, [128, 512], BF16) for _ in range(2)]
            vb = [sb(st, [128, 256], BF16) for _ in range(2)]
            cnt = 0

            def loadw(cb):
                S.dma(wst[:], od_w_in[i, :, cb * 256:(cb + 1) * 256].rearrange("(k p) c -> p k c", p=128))
                S.copy("pool", wbf[cb % 2][:], wst[:])

            loadw(0)
            for cb in range(ODIN // 256):
                if cb + 1 < ODIN // 256:
                    loadw(cb + 1)
                wb = wbf[cb % 2]
                c0 = cb * 256
                if 2560 <= c0 < 3072:
                    for tt in range(N // 128):
                        p = pp[cnt % 3]
                        for k in range(KC):
                            S.matmul(p[:, 0:256], lhsT=hT[k][:, tt * 128:(tt + 1) * 128], rhs=wb[:, k, :],
                                     start=(k == 0), stop=(k == KC - 1))
                        v_ = vb[cnt % 2]
                        S.copy("act" if cnt % 2 == 0 else "dve", v_[:], p[:, 0:256])
                        S.dma(VT[tt * 128:(tt + 1) * 128, c0 - 2560:c0 - 2560 + 256], v_[:], wkeys=[uk("VT")])
                        cnt += 1
                    continue
                for m in range(2):
                    col = c0 + m * 128
                    for (s0, ln, t0, w) in _blocks(L, T):
                        p = pp[cnt % 3]
                        z = cnt % 2
                        cnt += 1
                        for k in range(KC):
                            S.matmul(p[:, 0:w], lhsT=wb[:, k, m * 128:(m + 1) * 128], rhs=hT[k][:, t0:t0 + w],
                                     start=(k == 0), stop=(k == KC - 1))
                        if col >= 3072:
                            S.copy("act" if z == 0 else "dve", stg[z][:, 0:w], p[:, 0:w])
                            S.dma(PT[col - 3072:col - 3072 + 128, t0:t0 + w], stg[z][:, 0:w], wkeys=[uk("PT")])
                            continue
                        gcol = gq[:, 0:1] if col < 2048 else gq[:, 1:2]
                        S.act(xg[z][:, 0:w], p[:, 0:w], AF.Identity, scale=gcol)
                        S.act(sq[z][:, 0:w], p[:, 0:w], AF.Square)
                        S.matmul(pss[:, 0:w], lhsT=onr[:], rhs=sq[z][:, 0:w])
                        S.act(rs[z][:, 0:w], pss[:, 0:w], AF.Sqrt, bias=cst[:, 2:3], scale=1.0 / 128)
                        S.op("dve", "reciprocal", [rs[z][:]], [rs[z][:]], rs[z][:, 0:w], rs[z][:, 0:w])
                        if s0 == 0:
                            S.tt("dve", ob[z][:, 0:w], xg[z][:, 0:w].bitcast(F32), rs[z][:, 0:w], ALU.mult)
                        else:
                            S.matmul(prx[:, 0:w], lhsT=rtr[:], rhs=xg[z][:, 0:w])
                            S.tt("pool", t1[z][:, 0:w], xg[z][:, 0:w].bitcast(F32), cosT[:, t0 - L:t0 - L + w], ALU.mult)
                            S.tt("dve", t2[z][:, 0:w], prx[:, 0:w], sinT[:, t0 - L:t0 - L + w], ALU.mult)
                            S.tt("pool", t1[z][:, 0:w], t1[z][:, 0:w], t2[z][:, 0:w], ALU.add)
                            S.tt("dve", ob[z][:, 0:w], t1[z][:, 0:w], rs[z][:, 0:w], ALU.mult)
                        S.dma(QT[col:col + 128, t0:t0 + w], ob[z][:, 0:w], wkeys=[uk("QT")])
            S.flush()

    def phase_attn(b, l):
        last = l == depth - 1
        NKT = N // 128
        with contextlib.ExitStack() as st:
            kT = sb(st, [128, 4, N], BF16)
            V = sb(st, [128, NKT, 512], BF16)
            onesb = sb(st, [128, 128], BF16)
            S.dma(kT[:], QT[2048:2560, :].rearrange("(h p) t -> p h t", p=128))
            S.dma(V[:], VT[:, :].rearrange("(t p) c -> p t c", p=128))
            S.copy("dve", onesb[:], onesf[:])
            qT = [sb(st, [128, N], BF16) for _ in range(2)]
            gt = [sb(st, [128, N]) for _ in range(2)]
            psS = [ps(st, [128, 512]) for _ in range(3)]
            pex = [sb(st, [128, 512], BF16) for _ in range(3)]
            pO = [ps(st, [128, 512]) for _ in range(2)]
            pS = [ps(st, [128, 512]) for _ in range(2)]
            rsum = [sb(st, [128, 512]) for _ in range(2)]
            o1 = [sb(st, [128, 512]) for _ in range(2)]
            ob = [sb(st, [128, 512], BF16) for _ in range(2)]
            qblocks = []
            for (s0, ln, t0, w) in _blocks(L, T):
                if s0 == 0:
                    if not last:
                        qblocks.append((t0, w, list(range(L // 128))))
                else:
                    qblocks.append((t0, w, list(range(NKT))))
            items = []
            for h in range(16):
                for bi, (t0, w, kts) in enumerate(qblocks):
                    for ki, kt in enumerate(kts):
                        items.append((h, bi, ki))

            def issue_s(idx):
                h, bi, ki = items[idx]
                t0, w, kts = qblocks[bi]
                q_ = qT[h % 2]
                if bi == 0 and ki == 0:
                    g_ = gt[h % 2]
                    S.dma(q_[:], QT[h * 128:(h + 1) * 128, :])
                    S.dma(g_[:], PT[h * 128:(h + 1) * 128, :])
                    S.act(g_[:], g_[:], AF.Silu)
                kt = kts[ki]
                S.matmul(psS[idx % 3][:, 0:w], lhsT=kT[:, h // 4, kt * 128:(kt + 1) * 128], rhs=q_[:, t0:t0 + w])

            nblk = 0
            for idx in range(len(items)):
                if idx == 0:
                    issue_s(0)
                    if len(items) > 1:
                        issue_s(1)
                if idx + 2 < len(items):
                    issue_s(idx + 2)
                h, bi, ki = items[idx]
                kh = h // 4
                t0, w, kts = qblocks[bi]
                kt = kts[ki]
                z = nblk % 2
                pe2 = pex[idx % 3]
                S.act(pe2[:, 0:w], psS[idx % 3][:, 0:w], AF.Exp, scale=128 ** -0.5)
                S.matmul(pO[z][:, 0:w], lhsT=V[:, kt, kh * 128:(kh + 1) * 128], rhs=pe2[:, 0:w],
                         start=(ki == 0), stop=(ki == len(kts) - 1))
                S.matmul(pS[z][:, 0:w], lhsT=onesb[:], rhs=pe2[:, 0:w],
                         start=(ki == 0), stop=(ki == len(kts) - 1))
                if ki == len(kts) - 1:
                    g_ = gt[h % 2]
                    S.op("dve", "reciprocal", [pS[z][:]], [rsum[z][:]], rsum[z][:, 0:w], pS[z][:, 0:w])
                    S.tt("dve", o1[z][:, 0:w], pO[z][:, 0:w], rsum[z][:, 0:w], ALU.mult)
                    S.tt("pool", ob[z][:, 0:w], o1[z][:, 0:w], g_[:, t0:t0 + w], ALU.mult)
                    S.dma(MT[h * 128:(h + 1) * 128, t0:t0 + w], ob[z][:, 0:w], wkeys=[uk("MT")])
                    nblk += 1
            S.flush()

    def phase_out(b, l):
        i = l // 2
        last = l == depth - 1
        wout = ev_w_out if l % 2 == 0 else od_w_out
        src = xin if l == 0 else XS
        with contextlib.ExitStack() as st:
            gb = sb(st, [128, D])
            bbt = sb(st, [128, D])
            S.dma(gb[:], ln_g[l:l + 1, :].partition_broadcast(128))
            S.dma(bbt[:], ln_b[l:l + 1, :].partition_broadcast(128))
            xz2 = [sb(st, [128, 2, D]) for _ in range(2)]
            mt2 = [sb(st, [128, KC, 256], BF16) for _ in range(2)]
            wst2 = [sb(st, [128, KC, 256]) for _ in range(2)]
            walls = [sb(st, [128, KC, 256], BF16) for _ in range(D // 256)]
            for cb in range(D // 256):
                S.dma(wst2[cb % 2][:], wout[i, :, cb * 256:(cb + 1) * 256].rearrange("(k p) c -> p k c", p=128))
                S.copy("pool", walls[cb][:], wst2[cb % 2][:])
            pp = [ps(st, [128, 512]) for _ in range(2)]
            ptr = [ps(st, [128, 512]) for _ in range(2)]
            yg = [sb(st, [128, 256]) for _ in range(2)]
            junk = sb(st, [128, D], BF16)
            st2 = [sb(st, [128, 8]) for _ in range(2)]
            oo = [sb(st, [128, D]) for _ in range(2)]
            cnt = 0
            blks = [(s0, ln, t0, w) for (s0, ln, t0, w) in _blocks(L, T, bw=256) if not (last and s0 == 0)]

            def layer_norm_block(bi):
                s0, ln, t0, w = blks[bi]
                xz = xz2[bi % 2]
                for q in range(w // 128):
                    tt = t0 // 128 + q
                    zq = xz[:, q, :]
                    o_ = oo[q % 2]
                    st1 = st2[q % 2]
                    S.act(junk[:], zq, AF.Identity, accum_out=st1[:, 0:1])
                    S.ts("dve", st1[:, 1:2], st1[:, 0:1], -1.0 / D, ALU.mult)
                    S.act(junk[:], zq, AF.Square, bias=st1[:, 1:2], accum_out=st1[:, 2:3])
                    S.act(st1[:, 3:4], st1[:, 2:3], AF.Sqrt, bias=cst[:, 2:3], scale=1.0 / D)
                    S.op("dve", "reciprocal", [st1[:]], [st1[:]], st1[:, 4:5], st1[:, 3:4])
                    S.ts("dve", o_[:], zq, st1[:, 1:2], ALU.add, st1[:, 4:5], ALU.mult)
                    S.tt("pool", o_[:], o_[:], gb[:], ALU.mult)
                    S.tt("pool", o_[:], o_[:], bbt[:], ALU.add)
                    if last:
                        S.dma(yout[b, tt * 128 - L:(tt + 1) * 128 - L, :], o_[:], wkeys=[uk("Y")])
                    else:
                        S.dma(XS[b, tt * 128:(tt + 1) * 128, :], o_[:], wkeys=[("X", tt)])

            for bi, (s0, ln, t0, w) in enumerate(blks):
                nt = w // 128
                col = NB if s0 == 0 else b
                xz = xz2[bi % 2]
                mt = mt2[bi % 2]
                for q in range(nt):
                    tt = t0 // 128 + q
                    S.dma(xz[:, q, :], src[b, tt * 128:(tt + 1) * 128, :], rkeys=[("X", tt)], wkeys=[xz[:], ("xz", bi % 2, q)])
                S.dma(mt[:, :, 0:w], MT[:, t0:t0 + w].rearrange("(k p) t -> p k t", p=128))
                def mm(dj, z):
                    for k in range(KC):
                        S.matmul(pp[z][:, 0:w], lhsT=walls[dj // 2][:, k, (dj % 2) * 128:(dj % 2) * 128 + 128],
                                 rhs=mt[:, k, 0:w], start=(k == 0), stop=(k == KC - 1))

                mm(0, cnt % 2)
                for dj in range(D // 128):
                    if dj == 3 and bi > 0:
                        layer_norm_block(bi - 1)
                    z = cnt % 2
                    cnt += 1
                    p = pp[z]
                    if dj + 1 < D // 128:
                        mm(dj + 1, cnt % 2)
                    S.act(yg[z][:, 0:w], p[:, 0:w], AF.Identity, scale=modT[:, l, 32 + dj, col:col + 1])
                    for q in range(nt):
                        S.transpose(ptr[z][:, q * 128:(q + 1) * 128], yg[z][:, q * 128:(q + 1) * 128], ident[:])
                    S.stt(xz[:, 0:nt, dj * 128:(dj + 1) * 128], xz[:, 0:nt, dj * 128:(dj + 1) * 128], ALPHA,
                          ptr[z][:, 0:w].rearrange("p (q c) -> p q c", c=128), ALU.mult, ALU.add)
            layer_norm_block(len(blks) - 1)
            S.flush()

    nph = [0]

    def run(f, b, l):
        if nph[0] < upto:
            f(b, l)
        nph[0] += 1

    for b in range(NB):
        for l in range(depth):
            if l % 2 == 0:
                run(phase_inproj_even, b, l)
                run(phase_pool, b, l)
                run(phase_rwkv_prep, b, l)
                run(phase_rwkv_scan, b, l)
                run(phase_rwkv_out, b, l)
            else:
                run(phase_inproj_odd, b, l)
                run(phase_attn, b, l)
            run(phase_out, b, l)
    top.close()
    return nc


def host_consts(T, L):
    N = L + T
    f = np.float32
    c = {}
    c["c_ident"] = np.eye(128, dtype=f)
    rt = np.zeros((128, 128), f)
    for m in range(128):
        half = (m % 64) // 32
        if half == 0:
            rt[m + 32, m] = -1.0
        else:
            rt[m - 32, m] = 1.0
    c["c_rt"] = rt
    s = np.arange(64)[:, None]
    t = np.arange(64)[None, :]
    mk8 = np.zeros((64, 8, 4, 64), f)
    mn8 = np.zeros((64, 8, 64), f)
    for vh in range(8):
        fwd = vh < 4
        strict = (s < t) if fwd else (s > t)
        incl = (s <= t) if fwd else (s >= t)
        mk8[:, vh, 0] = strict
        mk8[:, vh, 1] = incl
        mk8[:, vh, 2] = strict
        mk8[:, vh, 3] = incl
        mn8[:, vh] = (t < s) if fwd else (t > s)
    c["c_mk8"] = mk8
    c["c_mn8"] = mn8
    rows = np.repeat(np.arange(T // 64, dtype=f), 64)
    cols = np.tile(np.arange(64, dtype=f), T // 64)
    inv = (10000.0 ** (-np.arange(0, 64, 2, dtype=f) / 64)).astype(f)
    p = np.arange(128)
    axis = p // 64
    fr = p % 32
    pos = np.where(axis[:, None] == 0, rows[None, :], cols[None, :]).astype(f)
    ang = (pos * inv[fr][:, None]).astype(f)
    c["c_cos"] = np.cos(ang).astype(f)
    c["c_sin"] = np.sin(ang).astype(f)
    rc = np.zeros((4, N), f)
    for g, w in enumerate(POOLW):
        for (s0, ln) in ((0, L), (L, T)):
            tt = np.arange(ln)
            lo = np.clip(tt - w // 2, 0, ln)
            hi = np.clip(tt - w // 2 + w, 0, ln)
            rc[g, s0:s0 + ln] = 1.0 / (hi - lo).astype(f)
    c["c_rcnt"] = rc
    return c


def host_params(p):
    f = np.float32
    out = {}
    evp = np.zeros((64, 2, NPAR), f)
    for i in range(2):
        mu = p["rw_mu_rkv"][i].reshape(2, 3, 16, 64)
        for d in range(2):
            for q in range(3):
                evp[:, i, (d * 3 + q) * 16:(d * 3 + q) * 16 + 16] = mu[d, q].T
            evp[:, i, 96 + d * 16:96 + d * 16 + 16] = p["rw_w0"][i, d].reshape(16, 64).T
            evp[:, i, 128 + d * 16:128 + d * 16 + 16] = p["rw_a0"][i, d].reshape(16, 64).T
            evp[:, i, 208 + d * 16:208 + d * 16 + 16] = p["rw_gn_g"][i, d].reshape(16, 64).T
            evp[:, i, 240 + d * 16:240 + d * 16 + 16] = p["rw_gn_b"][i, d].reshape(16, 64).T
            for q in range(2):
                evp[:, i, 272 + d * 2 + q] = p["rw_mu_lora"][i, d, q]
        evp[:, i, 160:176] = p["rw_k_k"][i].reshape(16, 64).T
        evp[:, i, 176:192] = p["rw_k_a"][i].reshape(16, 64).T
        evp[:, i, 192:208] = p["rw_r_k"][i].T
    out["evp"] = evp
    evq = np.zeros((128, 2, NPQ), f)
    pk = lambda v: np.asarray(v, f).reshape(8, 128).T
    for i in range(2):
        for d in range(2):
            for q in range(3):
                evq[:, i, (d * 3 + q) * 8:(d * 3 + q) * 8 + 8] = pk(p["rw_mu_rkv"][i, d, q])
            evq[:, i, 48 + d * 8:56 + d * 8] = pk(p["rw_w0"][i, d])
            evq[:, i, 64 + d * 8:72 + d * 8] = pk(p["rw_a0"][i, d])
            evq[:, i, 104 + d * 8:112 + d * 8] = pk(p["rw_gn_g"][i, d])
            evq[:, i, 120 + d * 8:128 + d * 8] = pk(p["rw_gn_b"][i, d])
        evq[:, i, 80:88] = pk(p["rw_k_k"][i])
        evq[:, i, 88:96] = pk(p["rw_k_a"][i])
        evq[:, i, 96:104] = pk(p["rw_r_k"][i].reshape(-1))
    out["evq"] = evq
    out["ps128"] = np.ascontiguousarray(p["pool_scale"].reshape(2, 8, 128).transpose(2, 0, 1)).astype(f)
    qkg = np.zeros((128, 2, 2), f)
    qkg[:, :, 0] = p["q_norm_g"].T
    qkg[:, :, 1] = p["k_norm_g"].T
    out["qkg"] = qkg
    out["modbT"] = np.ascontiguousarray(p["mod_b"].reshape(4, 48, 128).transpose(2, 0, 1)).astype(f)
    for k in ("mod_w", "ln_g", "ln_b", "ev_w_in", "ev_w_out", "pool_w", "rw_w2", "rw_a2", "od_w_in", "od_w_out"):
        out[k] = np.ascontiguousarray(p[k], dtype=f)
    return out


def host_core_inputs(x, c, ctx, c_ctx, core, NB):
    f = np.float32
    b0 = core * NB
    xin = np.concatenate([ctx[b0:b0 + NB], x[b0:b0 + NB]], axis=1).astype(f)
    cc = np.concatenate([c[b0:b0 + NB], c_ctx[None, :]], axis=0)
    cT = np.ascontiguousarray(cc.reshape(NB + 1, KC, 128).transpose(2, 1, 0)).astype(f)
    return {"xin": np.ascontiguousarray(xin), "cT": cT}


_CACHE = {}


def kernel(**inputs):
    x = np.asarray(inputs["x"], np.float32)
    B, T, _ = x.shape
    L = inputs["ctx"].shape[1]
    ncores = 8
    NB = B // ncores
    key = (T, L, NB)
    if key not in _CACHE:
        _CACHE[key] = build(T, L, NB)
    nc = _CACHE[key]
    shared = dict(host_consts(T, L))
    shared.update(host_params({k: np.asarray(v) for k, v in inputs.items()}))
    c = np.asarray(inputs["c"], np.float32)
    ctx = np.asarray(inputs["ctx"], np.float32)
    c_ctx = np.asarray(inputs["c_ctx"], np.float32)
    in_maps = []
    for core in range(ncores):
        m = dict(shared)
        m.update(host_core_inputs(x, c, ctx, c_ctx, core, NB))
        in_maps.append(m)
    res = run_bass_kernel_spmd(nc, in_maps, core_ids=list(range(ncores)))
    return np.concatenate([r["yout"] for r in res.results], axis=0).astype(np.float32)
```
